# Optimizing a Trainium2 kernel written in Bass

```python
import math
import jax
import jax.numpy as jnp
from jax import lax
import numpy as np

D_MODEL = 1024
BATCH = 8
SEQ = 2048
DEPTH = 4
DEC_BATCH = 128
DEC_SEQ = 4
PAST_LEN = 16384
PAGE_SIZE = 128

N_MIXERS = 3
N_S5_LAYERS = (DEPTH + 2) // 3
N_LRU_LAYERS = (DEPTH + 1) // 3
N_GDN_LAYERS = DEPTH // 3
RMS_EPS = 1e-6
L2_EPS = 1e-6
S5_WIDTH = D_MODEL
S5_GROUP_CH = 16
S5_GROUPS = S5_WIDTH // S5_GROUP_CH
S5_STATE = 64
S5_SCAN_CHUNK = 128
LRU_BLOCK = 128
LRU_WIDTH = (4 * D_MODEL // 3) // LRU_BLOCK * LRU_BLOCK
LRU_BLOCKS = LRU_WIDTH // LRU_BLOCK
LRU_C = 8.0
CONV_WIDTH = 4
GDN_DK = 128
GDN_DV = 128
GDN_HEADS = D_MODEL // 128
GDN_KEY_DIM = GDN_HEADS * GDN_DK
GDN_VAL_DIM = GDN_HEADS * GDN_DV
GDN_CONV_DIM = 2 * GDN_KEY_DIM + GDN_VAL_DIM
GDN_PROJ_DIM = GDN_CONV_DIM + GDN_VAL_DIM + 2 * GDN_HEADS
GDN_CHUNK = 64
FFN_HIDDEN = ((8 * D_MODEL // 3 + 127) // 128) * 128
FFN_CONV_WIDTH = 3

kernel_name = 'hybrid_s5_rglru_gdn_convffn_step'


def rmsnorm(x, g):
    xf = x.astype(jnp.float32)
    y = xf * lax.rsqrt(jnp.mean(xf * xf, axis=-1, keepdims=True) + RMS_EPS)
    return (y * g.astype(jnp.float32)).astype(x.dtype)


def l2norm(x):
    xf = x.astype(jnp.float32)
    return xf * lax.rsqrt(jnp.sum(xf * xf, axis=-1, keepdims=True) + L2_EPS)


def causal_dwconv(x, buf, w, b=None):
    width, seq = w.shape[0], x.shape[1]
    xp = jnp.concatenate([buf.astype(x.dtype), x], axis=1)
    y = xp[:, 0:seq] * w[0]
    for k in range(1, width):
        y = y + xp[:, k:k + seq] * w[k]
    if b is not None:
        y = y + b
    return y, xp[:, seq:]


def linear_scan(a, b, h0):
    b = b.at[:, 0].add(a[:, 0] * h0)

    def combine(left, right):
        a_l, b_l = left
        a_r, b_r = right
        return a_r * a_l, a_r * b_l + b_r

    _, h = lax.associative_scan(combine, (a, b), axis=1)
    return h


def s5_mixer(x, h0_re, h0_im, w_in, a_re, a_im, log_dt, b_re, b_im, c_re, c_im, d, w_glu):
    bsz, seq, _ = x.shape
    f32 = jnp.float32
    u = (x @ w_in).astype(f32)
    lam = lax.complex(a_re.astype(f32), a_im.astype(f32))
    dt = jnp.exp(log_dt.astype(f32))[:, None]
    a_bar = jnp.exp(lam * dt)
    b_bar = ((a_bar - 1.0) / lam)[..., None] * lax.complex(b_re.astype(f32), b_im.astype(f32))
    c_mat = lax.complex(c_re.astype(f32), c_im.astype(f32))
    h0 = lax.complex(h0_re.astype(f32), h0_im.astype(f32))
    chunk = math.gcd(seq, S5_SCAN_CHUNK)
    n_chunks = seq // chunk
    u_chunks = u.reshape(bsz, n_chunks, chunk, S5_GROUPS, S5_GROUP_CH).transpose(1, 0, 2, 3, 4)

    def step(h, u_c):
        bu = jnp.einsum('blgc,gpc->blgp', u_c.astype(jnp.complex64), b_bar)
        hs = linear_scan(jnp.broadcast_to(a_bar, bu.shape), bu, h)
        y_c = jnp.einsum('blgp,gcp->blgc', hs, c_mat).real
        return hs[:, -1], y_c

    h_last, ys = lax.scan(step, h0, u_chunks)
    y = ys.transpose(1, 0, 2, 3, 4).reshape(bsz, seq, S5_WIDTH) + d.astype(f32) * u
    y = jax.nn.gelu(y).astype(x.dtype)
    val, gate = jnp.split(y @ w_glu, 2, axis=-1)
    return val * jax.nn.sigmoid(gate), jnp.real(h_last), jnp.imag(h_last)


def rglru_mixer(x, h0, conv_buf, w_in, conv_w, conv_b, w_ga, b_ga, w_gx, b_gx, lam, w_out):
    bsz, seq, _ = x.shape
    f32 = jnp.float32
    gate_br, x_br = jnp.split(x @ w_in, 2, axis=-1)
    xc, new_buf = causal_dwconv(x_br, conv_buf, conv_w, conv_b)
    xb = xc.reshape(bsz, seq, LRU_BLOCKS, LRU_BLOCK)
    r = jax.nn.sigmoid(jnp.einsum('blnd,nde->blne', xb, w_ga).reshape(bsz, seq, LRU_WIDTH).astype(f32) + b_ga.astype(f32))
    i = jax.nn.sigmoid(jnp.einsum('blnd,nde->blne', xb, w_gx).reshape(bsz, seq, LRU_WIDTH).astype(f32) + b_gx.astype(f32))
    log_a = -LRU_C * r * jax.nn.softplus(-lam.astype(f32))
    a = jnp.exp(log_a)
    b = jnp.sqrt(-jnp.expm1(2.0 * log_a)) * i * xc.astype(f32)
    h = linear_scan(a, b, h0.astype(f32))
    y = (jax.nn.gelu(gate_br.astype(f32)) * h).astype(x.dtype)
    return y @ w_out, h[:, -1], new_buf


def chunked_gated_delta(q, k, v, g, beta, s0):
    bsz, seq, heads, dk = q.shape
    dv = v.shape[-1]
    chunk = math.gcd(seq, GDN_CHUNK)
    n_chunks = seq // chunk

    def blocks(t):
        return t.reshape(bsz, n_chunks, chunk, heads, t.shape[-1]).transpose(1, 0, 3, 2, 4)

    def blocks_h(t):
        return t.reshape(bsz, n_chunks, chunk, heads).transpose(1, 0, 3, 2)

    causal = jnp.tril(jnp.ones((chunk, chunk), dtype=bool))
    strict = jnp.tril(jnp.ones((chunk, chunk), dtype=bool), k=-1)
    eye = jnp.eye(chunk, dtype=jnp.float32)

    def step(s, inp):
        q_c, k_c, v_c, g_c, b_c = inp
        cum = jnp.cumsum(g_c, axis=-1)
        decay = jnp.exp(jnp.where(causal, cum[..., :, None] - cum[..., None, :], -jnp.inf))
        k_beta = k_c * b_c[..., None]
        m = jnp.where(strict, jnp.einsum('bhid,bhjd->bhij', k_beta, k_c) * decay, 0.0)
        rhs = jnp.concatenate([v_c * b_c[..., None], k_beta * jnp.exp(cum)[..., None]], axis=-1)
        sol = lax.linalg.triangular_solve(m + eye, rhs, left_side=True, lower=True, unit_diagonal=True)
        u_c, w_c = sol[..., :dv], sol[..., dv:]
        v_new = u_c - jnp.einsum('bhck,bhkv->bhcv', w_c, s)
        attn = jnp.einsum('bhik,bhjk->bhij', q_c, k_c) * decay
        o_c = jnp.einsum('bhck,bhkv->bhcv', q_c * jnp.exp(cum)[..., None], s) + jnp.einsum('bhij,bhjv->bhiv', attn, v_new)
        g_last = cum[..., -1:]
        s_new = s * jnp.exp(g_last)[..., None] + jnp.einsum('bhck,bhcv->bhkv', k_c * jnp.exp(g_last - cum)[..., None], v_new)
        return s_new, o_c

    s_last, o = lax.scan(step, s0, (blocks(q), blocks(k), blocks(v), blocks_h(g), blocks_h(beta)))
    return o.transpose(1, 0, 3, 2, 4).reshape(bsz, seq, heads, dv), s_last


def gdn_mixer(x, s0, conv_buf, w_in, conv_w, a_log, dt_bias, norm_w, w_out):
    bsz, seq, _ = x.shape
    f32 = jnp.float32
    proj = x @ w_in
    qkv, z, a_in, b_in = jnp.split(proj, [GDN_CONV_DIM, GDN_CONV_DIM + GDN_VAL_DIM, GDN_CONV_DIM + GDN_VAL_DIM + GDN_HEADS], axis=-1)
    qkv, new_buf = causal_dwconv(qkv, conv_buf, conv_w)
    qkv = jax.nn.silu(qkv)
    q, k, v = jnp.split(qkv, [GDN_KEY_DIM, 2 * GDN_KEY_DIM], axis=-1)
    q = l2norm(q.reshape(bsz, seq, GDN_HEADS, GDN_DK)) * (GDN_DK ** -0.5)
    k = l2norm(k.reshape(bsz, seq, GDN_HEADS, GDN_DK))
    v = v.reshape(bsz, seq, GDN_HEADS, GDN_DV).astype(f32)
    beta = jax.nn.sigmoid(b_in.astype(f32))
    g = -jnp.exp(a_log.astype(f32)) * jax.nn.softplus(a_in.astype(f32) + dt_bias.astype(f32))
    o, s_last = chunked_gated_delta(q, k, v, g, beta, s0.astype(f32))
    o = rmsnorm(o, norm_w) * jax.nn.silu(z.reshape(bsz, seq, GDN_HEADS, GDN_DV).astype(f32))
    return o.reshape(bsz, seq, GDN_VAL_DIM).astype(x.dtype) @ w_out, s_last, new_buf


def conv_ffn(x, buf, w_up, conv_w, conv_b, w_down):
    h, new_buf = causal_dwconv(x @ w_up, buf, conv_w, conv_b)
    a, b = jnp.split(h, 2, axis=-1)
    return (jax.nn.gelu(a) * b) @ w_down, new_buf


def trunk(x, s5_re, s5_im, lru_h, lru_conv, gdn_s, gdn_conv, ffn_conv, p):
    o_s5_re, o_s5_im, o_lru, o_lru_conv, o_gdn, o_gdn_conv, o_ffn_conv = [], [], [], [], [], [], []
    for i in range(DEPTH):
        kind, j = i % N_MIXERS, i // N_MIXERS
        xn = rmsnorm(x, p['norm_mix'][i])
        if kind == 0:
            y, hr, hi = s5_mixer(xn, s5_re[j], s5_im[j], p['s5_w_in'][j], p['s5_a_re'][j], p['s5_a_im'][j],
                                 p['s5_log_dt'][j], p['s5_b_re'][j], p['s5_b_im'][j], p['s5_c_re'][j],
                                 p['s5_c_im'][j], p['s5_d'][j], p['s5_w_glu'][j])
            o_s5_re.append(hr)
            o_s5_im.append(hi)
        elif kind == 1:
            y, h, buf = rglru_mixer(xn, lru_h[j], lru_conv[j], p['lru_w_in'][j], p['lru_conv_w'][j], p['lru_conv_b'][j],
                                    p['lru_w_gate_a'][j], p['lru_b_gate_a'][j], p['lru_w_gate_x'][j],
                                    p['lru_b_gate_x'][j], p['lru_lambda'][j], p['lru_w_out'][j])
            o_lru.append(h)
            o_lru_conv.append(buf)
        else:
            y, s, buf = gdn_mixer(xn, gdn_s[j], gdn_conv[j], p['gdn_w_in'][j], p['gdn_conv_w'][j], p['gdn_a_log'][j],
                                  p['gdn_dt_bias'][j], p['gdn_norm'][j], p['gdn_w_out'][j])
            o_gdn.append(s)
            o_gdn_conv.append(buf)
        x = x + y.astype(x.dtype)
        xn = rmsnorm(x, p['norm_ffn'][i])
        y, buf = conv_ffn(xn, ffn_conv[i], p['ffn_w_up'][i], p['ffn_conv_w'][i], p['ffn_conv_b'][i], p['ffn_w_down'][i])
        o_ffn_conv.append(buf)
        x = x + y.astype(x.dtype)
    x = rmsnorm(x, p['norm_final'])
    return (x, jnp.stack(o_s5_re), jnp.stack(o_s5_im), jnp.stack(o_lru), jnp.stack(o_lru_conv),
            jnp.stack(o_gdn), jnp.stack(o_gdn_conv), jnp.stack(o_ffn_conv))


def setup_inputs(seed: int = 0) -> dict:
    key = jax.random.key(seed)
    keys = jax.random.split(key, 64)
    counter = [0]

    def nk():
        counter[0] += 1
        return keys[counter[0] - 1]

    def nrm(shape, scale):
        return scale * jax.random.normal(nk(), shape, jnp.float32)

    def unif(shape, lo, hi):
        return jax.random.uniform(nk(), shape, jnp.float32, lo, hi)

    f2 = 2 * FFN_HIDDEN
    lru_s = unif((N_LRU_LAYERS, LRU_WIDTH), 0.9, 0.999) ** (1.0 / LRU_C)
    gdn_dt = jnp.exp(unif((N_GDN_LAYERS, GDN_HEADS), math.log(1e-3), math.log(1e-1)))
    return {
        'x_prompt': nrm((BATCH, SEQ, D_MODEL), 1.0),
        'x_sample': nrm((DEC_BATCH, DEC_SEQ, D_MODEL), 1.0),
        'state_s5_re': nrm((N_S5_LAYERS, DEC_BATCH, S5_GROUPS, S5_STATE), 0.1),
        'state_s5_im': nrm((N_S5_LAYERS, DEC_BATCH, S5_GROUPS, S5_STATE), 0.1),
        'state_lru': nrm((N_LRU_LAYERS, DEC_BATCH, LRU_WIDTH), 0.5),
        'state_lru_conv': nrm((N_LRU_LAYERS, DEC_BATCH, CONV_WIDTH - 1, LRU_WIDTH), 1.0),
        'state_gdn': nrm((N_GDN_LAYERS, DEC_BATCH, GDN_HEADS, GDN_DK, GDN_DV), 0.1),
        'state_gdn_conv': nrm((N_GDN_LAYERS, DEC_BATCH, CONV_WIDTH - 1, GDN_CONV_DIM), 1.0),
        'state_ffn_conv': nrm((DEPTH, DEC_BATCH, FFN_CONV_WIDTH - 1, f2), 1.0),
        'norm_mix': 1.0 + nrm((DEPTH, D_MODEL), 0.02),
        'norm_ffn': 1.0 + nrm((DEPTH, D_MODEL), 0.02),
        'norm_final': 1.0 + nrm((D_MODEL,), 0.02),
        's5_w_in': nrm((N_S5_LAYERS, D_MODEL, S5_WIDTH), D_MODEL ** -0.5),
        's5_a_re': -0.5 + nrm((N_S5_LAYERS, S5_GROUPS, S5_STATE), 0.01),
        's5_a_im': jnp.pi * jnp.arange(S5_STATE, dtype=jnp.float32) + nrm((N_S5_LAYERS, S5_GROUPS, S5_STATE), 0.01),
        's5_log_dt': unif((N_S5_LAYERS, S5_GROUPS), math.log(1e-3), math.log(1e-1)),
        's5_b_re': nrm((N_S5_LAYERS, S5_GROUPS, S5_STATE, S5_GROUP_CH), (2 * S5_GROUP_CH) ** -0.5),
        's5_b_im': nrm((N_S5_LAYERS, S5_GROUPS, S5_STATE, S5_GROUP_CH), (2 * S5_GROUP_CH) ** -0.5),
        's5_c_re': nrm((N_S5_LAYERS, S5_GROUPS, S5_GROUP_CH, S5_STATE), S5_STATE ** -0.5),
        's5_c_im': nrm((N_S5_LAYERS, S5_GROUPS, S5_GROUP_CH, S5_STATE), S5_STATE ** -0.5),
        's5_d': nrm((N_S5_LAYERS, S5_WIDTH), 1.0),
        's5_w_glu': nrm((N_S5_LAYERS, S5_WIDTH, 2 * D_MODEL), S5_WIDTH ** -0.5),
        'lru_w_in': nrm((N_LRU_LAYERS, D_MODEL, 2 * LRU_WIDTH), D_MODEL ** -0.5),
        'lru_conv_w': nrm((N_LRU_LAYERS, CONV_WIDTH, LRU_WIDTH), CONV_WIDTH ** -0.5),
        'lru_conv_b': nrm((N_LRU_LAYERS, LRU_WIDTH), 0.01),
        'lru_w_gate_a': nrm((N_LRU_LAYERS, LRU_BLOCKS, LRU_BLOCK, LRU_BLOCK), LRU_BLOCK ** -0.5),
        'lru_b_gate_a': nrm((N_LRU_LAYERS, LRU_WIDTH), 0.01),
        'lru_w_gate_x': nrm((N_LRU_LAYERS, LRU_BLOCKS, LRU_BLOCK, LRU_BLOCK), LRU_BLOCK ** -0.5),
        'lru_b_gate_x': nrm((N_LRU_LAYERS, LRU_WIDTH), 0.01),
        'lru_lambda': jnp.log(lru_s) - jnp.log1p(-lru_s),
        'lru_w_out': nrm((N_LRU_LAYERS, LRU_WIDTH, D_MODEL), LRU_WIDTH ** -0.5),
        'gdn_w_in': nrm((N_GDN_LAYERS, D_MODEL, GDN_PROJ_DIM), D_MODEL ** -0.5),
        'gdn_conv_w': nrm((N_GDN_LAYERS, CONV_WIDTH, GDN_CONV_DIM), CONV_WIDTH ** -0.5),
        'gdn_a_log': jnp.log(unif((N_GDN_LAYERS, GDN_HEADS), 1.0, 16.0)),
        'gdn_dt_bias': gdn_dt + jnp.log(-jnp.expm1(-gdn_dt)),
        'gdn_norm': 1.0 + nrm((N_GDN_LAYERS, GDN_DV), 0.02),
        'gdn_w_out': nrm((N_GDN_LAYERS, GDN_VAL_DIM, D_MODEL), GDN_VAL_DIM ** -0.5),
        'ffn_w_up': nrm((DEPTH, D_MODEL, f2), D_MODEL ** -0.5),
        'ffn_conv_w': nrm((DEPTH, FFN_CONV_WIDTH, f2), FFN_CONV_WIDTH ** -0.5),
        'ffn_conv_b': nrm((DEPTH, f2), 0.01),
        'ffn_w_down': nrm((DEPTH, FFN_HIDDEN, D_MODEL), FFN_HIDDEN ** -0.5),
    }


def reference(x_prompt, x_sample, state_s5_re, state_s5_im, state_lru, state_lru_conv, state_gdn, state_gdn_conv,
              state_ffn_conv, norm_mix, norm_ffn, norm_final, s5_w_in, s5_a_re, s5_a_im, s5_log_dt, s5_b_re, s5_b_im,
              s5_c_re, s5_c_im, s5_d, s5_w_glu, lru_w_in, lru_conv_w, lru_conv_b, lru_w_gate_a, lru_b_gate_a,
              lru_w_gate_x, lru_b_gate_x, lru_lambda, lru_w_out, gdn_w_in, gdn_conv_w, gdn_a_log, gdn_dt_bias,
              gdn_norm, gdn_w_out, ffn_w_up, ffn_conv_w, ffn_conv_b, ffn_w_down):
    p = {
        'norm_mix': norm_mix, 'norm_ffn': norm_ffn, 'norm_final': norm_final,
        's5_w_in': s5_w_in, 's5_a_re': s5_a_re, 's5_a_im': s5_a_im, 's5_log_dt': s5_log_dt,
        's5_b_re': s5_b_re, 's5_b_im': s5_b_im, 's5_c_re': s5_c_re, 's5_c_im': s5_c_im,
        's5_d': s5_d, 's5_w_glu': s5_w_glu,
        'lru_w_in': lru_w_in, 'lru_conv_w': lru_conv_w, 'lru_conv_b': lru_conv_b,
        'lru_w_gate_a': lru_w_gate_a, 'lru_b_gate_a': lru_b_gate_a, 'lru_w_gate_x': lru_w_gate_x,
        'lru_b_gate_x': lru_b_gate_x, 'lru_lambda': lru_lambda, 'lru_w_out': lru_w_out,
        'gdn_w_in': gdn_w_in, 'gdn_conv_w': gdn_conv_w, 'gdn_a_log': gdn_a_log, 'gdn_dt_bias': gdn_dt_bias,
        'gdn_norm': gdn_norm, 'gdn_w_out': gdn_w_out,
        'ffn_w_up': ffn_w_up, 'ffn_conv_w': ffn_conv_w, 'ffn_conv_b': ffn_conv_b, 'ffn_w_down': ffn_w_down,
    }
    bsz = x_prompt.shape[0]
    f32 = jnp.float32
    dt = x_prompt.dtype
    (y_prompt, p_s5_re, p_s5_im, p_lru, p_lru_conv, p_gdn, p_gdn_conv, p_ffn_conv) = trunk(
        x_prompt,
        jnp.zeros((N_S5_LAYERS, bsz, S5_GROUPS, S5_STATE), f32),
        jnp.zeros((N_S5_LAYERS, bsz, S5_GROUPS, S5_STATE), f32),
        jnp.zeros((N_LRU_LAYERS, bsz, LRU_WIDTH), f32),
        jnp.zeros((N_LRU_LAYERS, bsz, CONV_WIDTH - 1, LRU_WIDTH), dt),
        jnp.zeros((N_GDN_LAYERS, bsz, GDN_HEADS, GDN_DK, GDN_DV), f32),
        jnp.zeros((N_GDN_LAYERS, bsz, CONV_WIDTH - 1, GDN_CONV_DIM), dt),
        jnp.zeros((DEPTH, bsz, FFN_CONV_WIDTH - 1, 2 * FFN_HIDDEN), dt),
        p)
    (y_sample, s_s5_re, s_s5_im, s_lru, s_lru_conv, s_gdn, s_gdn_conv, s_ffn_conv) = trunk(
        x_sample, state_s5_re, state_s5_im, state_lru, state_lru_conv, state_gdn, state_gdn_conv,
        state_ffn_conv, p)
    return (y_prompt, y_sample, p_s5_re, p_s5_im, p_lru, p_lru_conv, p_gdn, p_gdn_conv, p_ffn_conv,
            s_s5_re, s_s5_im, s_lru, s_lru_conv, s_gdn, s_gdn_conv, s_ffn_conv)
```

```python
import numpy as np
import concourse.bass as bass
import concourse.mybir as mybir

F32 = mybir.dt.float32
BF16 = mybir.dt.bfloat16
I32 = mybir.dt.int32
AF = mybir.ActivationFunctionType
ALU = mybir.AluOpType
AX = mybir.AxisListType

ENGS = ("pe", "act", "dve", "pool", "sp")
_DSZ = {F32: 4, BF16: 2, I32: 4}
SEM_BLK = 2000
N_DMA_SEMS = 8
EMBED_WAIT = True


class Op:
    __slots__ = ("eng", "fn", "idx", "deps", "sig", "gcount", "is_dma", "dsem", "dval", "tag")

    def __init__(self, eng, fn, idx, is_dma, tag):
        self.eng = eng
        self.fn = fn
        self.idx = idx
        self.deps = set()
        self.sig = False
        self.gcount = 0
        self.is_dma = is_dma
        self.dsem = None
        self.dval = 0
        self.tag = tag


class Prog:
    def __init__(self, nc):
        self.nc = nc
        self.q = {e: [] for e in ENGS}
        self.recs = {}
        self.dma_count = {e: 0 for e in ENGS}
        self.dma_hist = {e: [] for e in ENGS}

    @staticmethod
    def _acc(ap):
        t = ap.tensor
        aps = ap.ap
        off = int(ap.offset)
        if isinstance(t, bass.DRamTensorHandle) or "DRam" in type(t).__name__:
            return None
        pstep, pcnt = aps[0]
        if pstep == 0:
            pstep = 1 << 40
        plo = off // pstep
        flo = off % pstep
        fhi = flo + sum((c - 1) * abs(s) for s, c in aps[1:]) + 1
        dsz = _DSZ[ap.dtype]
        if "PSum" in type(t).__name__:
            return (t.name, plo // 32 * 32, (plo + pcnt + 31) // 32 * 32, 0, 2048)
        return (t.name, plo, plo + pcnt, flo * dsz, fhi * dsz)

    def add(self, eng, fn, reads=(), writes=(), dma=False, tag=None):
        op = Op(eng, fn, len(self.q[eng]), dma, tag)
        deps = op.deps
        racc = [a for a in (self._acc(r) for r in reads) if a is not None]
        wacc = [a for a in (self._acc(w) for w in writes) if a is not None]
        for (name, plo, phi, flo, fhi) in racc:
            for r in self.recs.get(name, ()):
                if r[5] and r[0] < phi and plo < r[1] and r[2] < fhi and flo < r[3]:
                    deps.add(r[4])
        for (name, plo, phi, flo, fhi) in wacc:
            for r in self.recs.get(name, ()):
                if r[0] < phi and plo < r[1] and r[2] < fhi and flo < r[3]:
                    deps.add(r[4])
        deps.discard(op)
        for (name, plo, phi, flo, fhi) in wacc:
            lst = self.recs.setdefault(name, [])
            new = []
            for r in lst:
                if r[0] < phi and plo < r[1] and r[2] < fhi and flo < r[3] and r[0] >= plo and r[1] <= phi:
                    if r[2] < flo:
                        new.append([r[0], r[1], r[2], flo, r[4], r[5]])
                    if r[3] > fhi:
                        new.append([r[0], r[1], fhi, r[3], r[4], r[5]])
                else:
                    new.append(r)
            new.append([plo, phi, flo, fhi, op, True])
            self.recs[name] = new
        for (name, plo, phi, flo, fhi) in racc:
            lst = self.recs.setdefault(name, [])
            if not dma:
                lst = [r for r in lst if not ((not r[5]) and r[4].eng == eng and (not r[4].is_dma)
                                              and r[0] >= plo and r[1] <= phi and r[2] >= flo and r[3] <= fhi)]
            lst.append([plo, phi, flo, fhi, op, False])
            self.recs[name] = lst
        if dma:
            hist = self.dma_hist[eng]
            k = len(hist)
            if k >= N_DMA_SEMS:
                deps.add(hist[k - N_DMA_SEMS])
            hist.append(op)
        self.q[eng].append(op)
        return op

    def emit(self, block_engines):
        nc = self.nc
        for e in ENGS:
            for op in self.q[e]:
                for d in op.deps:
                    if d.is_dma:
                        continue
                    if d.eng == "pe" and op.eng == "pe" and not op.is_dma:
                        continue
                    d.sig = True
        nsig = {}
        for e in ENGS:
            n = 0
            for op in self.q[e]:
                if op.is_dma:
                    continue
                if op.sig:
                    n += 1
                    op.gcount = n
            nsig[e] = n
        return nsig


def run_prog(nc, P, final_wait=True):
    import contextlib
    nsig = P.emit(None)
    with contextlib.ExitStack() as st:
        sems = {}
        for e in ENGS:
            nb = max(1, (nsig[e] + SEM_BLK - 1) // SEM_BLK)
            sems[e] = [st.enter_context(nc.semaphore(f"s_{e}_{i}")) for i in range(nb)]
        dsems = {}
        for e in ENGS:
            if P.dma_hist[e]:
                dsems[e] = [st.enter_context(nc.semaphore(f"d_{e}_{i}")) for i in range(N_DMA_SEMS)]
                for k, op in enumerate(P.dma_hist[e]):
                    op.dsem = dsems[e][k % N_DMA_SEMS]
                    op.dval = 16 * (k // N_DMA_SEMS + 1)
        block = st.enter_context(nc.Block())

        def make(e):
            def body(eng):
                waited_c = {}
                waited_d = {}
                for op in P.q[e]:
                    need_c = {}
                    for d in op.deps:
                        if d.is_dma:
                            key = id(d.dsem)
                            if waited_d.get(key, 0) < d.dval:
                                waited_d[key] = d.dval
                                eng.wait_ge(d.dsem, d.dval)
                        else:
                            if d.eng == "pe" and e == "pe" and not op.is_dma:
                                continue
                            if d.gcount > need_c.get(d.eng, 0):
                                need_c[d.eng] = d.gcount
                    pend = []
                    for se, g in need_c.items():
                        if waited_c.get(se, 0) < g:
                            waited_c[se] = g
                            b = (g - 1) // SEM_BLK
                            pend.append((sems[se][b], (g - 1) % SEM_BLK + 1))
                    emb = None
                    if EMBED_WAIT and pend and not op.is_dma and op.tag == "w":
                        emb = pend.pop()
                    for s_, v_ in pend:
                        eng.wait_ge(s_, v_)
                    ins = op.fn(eng)
                    if emb is not None:
                        ins._wait_ge(emb[0], emb[1])
                    if op.is_dma:
                        ins.then_inc(op.dsem, 16)
                    elif op.sig:
                        b = (op.gcount - 1) // SEM_BLK
                        ins.then_inc(sems[e][b], 1)
                if e == "sp" and final_wait:
                    for qe in ENGS:
                        hist = P.dma_hist[qe]
                        for k in range(max(0, len(hist) - N_DMA_SEMS), len(hist)):
                            op = hist[k]
                            eng.wait_ge(op.dsem, op.dval)
            return body

        block.tensor(make("pe"))
        block.scalar(make("act"))
        block.vector(make("dve"))
        block.gpsimd(make("pool"))
        block.sync(make("sp"))


def _aps(*xs):
    return [x for x in xs if x is not None and not isinstance(x, (int, float))]


class PX(Prog):
    def mm(self, out, lhsT, rhs, start=True, stop=True, tag=None, **kw):
        if tag is None and getattr(self, "pe_embed", False) and lhsT.dtype == BF16 and not kw:
            tag = "w"
        return self.add("pe", lambda e: e.matmul(out, lhsT, rhs, start=start, stop=stop, **kw),
                        reads=[lhsT, rhs], writes=[out], tag=tag)

    def tr(self, out, in_, ident):
        return self.add("pe", lambda e: e.transpose(out, in_, ident), reads=[in_, ident], writes=[out])

    def act(self, out, in_, func, bias=None, scale=None, eng="act"):
        kw = {}
        if bias is not None:
            kw["bias"] = bias
        if scale is not None:
            kw["scale"] = scale
        return self.add(eng, lambda e: e.activation(out, in_, func, **kw),
                        reads=_aps(in_, bias, scale), writes=[out], tag=("w" if not _aps(bias, scale) else None))

    def tt(self, eng, out, in0, in1, op):
        return self.add(eng, lambda e: e.tensor_tensor(out, in0, in1, op), reads=[in0, in1], writes=[out], tag="w")

    def ts(self, eng, out, in0, s1, s2, op0, op1=None):
        if op1 is None:
            return self.add(eng, lambda e: e.tensor_scalar(out, in0, s1, None, op0),
                            reads=_aps(in0, s1), writes=[out], tag=("w" if not _aps(s1) else None))
        return self.add(eng, lambda e: e.tensor_scalar(out, in0, s1, s2, op0, op1),
                        reads=_aps(in0, s1, s2), writes=[out], tag=("w" if not _aps(s1, s2) else None))

    def stt(self, eng, out, in0, scalar, in1, op0, op1):
        return self.add(eng, lambda e: e.scalar_tensor_tensor(out, in0, scalar, in1, op0, op1),
                        reads=_aps(in0, scalar, in1), writes=[out], tag=("w" if not _aps(scalar) else None))

    def copy(self, eng, out, in_):
        if eng == "act":
            return self.add(eng, lambda e: e.copy(out, in_), reads=[in_], writes=[out], tag="w")
        return self.add(eng, lambda e: e.tensor_copy(out, in_), reads=[in_], writes=[out], tag="w")

    def scan(self, eng, out, d0, d1, init, op0, op1):
        return self.add(eng, lambda e: e.tensor_tensor_scan(out, d0, d1, init, op0, op1),
                        reads=_aps(d0, d1, init), writes=[out], tag=("w" if not _aps(init) else None))

    def memset(self, eng, ap, val):
        return self.add(eng, lambda e: e.memset(ap, val), writes=[ap])

    def dma(self, eng, out, in_, **kw):
        return self.add(eng, lambda e: e.dma_start(out=out, in_=in_, **kw), reads=[in_], writes=[out], dma=True)

    def recip(self, eng, out, in_):
        return self.add(eng, lambda e: e.reciprocal(out, in_), reads=[in_], writes=[out])


import contextlib
from concourse.bass_utils import run_bass_kernel_spmd

NTP = 2048
NSB = 16
NST = 4
NS = NSB * NST
NT = NTP + NS
D = 1024
DC = 8
FH = 2816
F2 = 5632
NPAIR = 22
DEPTH = 4
LRU_W = 1280
LRU_C = 10
GQ = 3072
EPS = 1e-6

TILES = [(i * 512, 512, "p") for i in range(4)] + [(NTP, NS, "s")]


class K:
    pass


def build(layers_cfg=None, n_depth=DEPTH, dbg=''):
    nc = bass.Bass("TRN2", target_bir_lowering=False)
    k = K()
    k.nc = nc
    P = PX(nc)
    k.P = P
    st = contextlib.ExitStack()
    k.st = st

    def din(name, shape):
        return nc.dram_tensor(name, list(shape), F32, kind="ExternalInput").ap()

    def dout(name, shape):
        return nc.dram_tensor(name, list(shape), F32, kind="ExternalOutput").ap()

    def sb(name, shape, dt=F32):
        return st.enter_context(nc.sbuf_tensor(name, list(shape), dt))

    k.sb = sb
    ARENA_BYTES = 93 * 1024
    k.AR = sb("AR", [128, ARENA_BYTES // 4], F32)
    k.ar_off = 0

    def alloc(shape, dt=F32, at=None):
        dsz = _DSZ[dt]
        n = 1
        for d_ in shape[1:]:
            n *= d_
        nbytes = (n * dsz + 31) // 32 * 32
        off = k.ar_off if at is None else at
        assert off + nbytes <= ARENA_BYTES, (off, nbytes, shape)
        if at is None:
            k.ar_off = off + nbytes
        v = k.AR[:, off // 4:(off + nbytes) // 4]
        if dt != F32:
            v = v.bitcast(dt)
        v = v[:, 0:n]
        if len(shape) > 2:
            names = " ".join(f"d{i}" for i in range(len(shape) - 1))
            kw = {f"d{i}": shape[i + 1] for i in range(len(shape) - 1)}
            v = v.rearrange(f"p ({names}) -> p {names}", **kw)
        if shape[0] < 128:
            v = v[0:shape[0]]
        return v

    k.alloc = alloc
    I = {}
    I["xp"] = din("xp", [NTP, D])
    I["xs"] = din("xs", [NS, D])
    I["st_s5_re"] = din("st_s5_re", [2, NSB, 64, 64])
    I["st_s5_im"] = din("st_s5_im", [2, NSB, 64, 64])
    I["st_lru"] = din("st_lru", [1, NSB, LRU_W])
    I["st_lru_conv"] = din("st_lru_conv", [1, NSB, 3, LRU_W])
    I["st_gdn"] = din("st_gdn", [1, NSB, 8, 128, 128])
    I["st_gdn_conv"] = din("st_gdn_conv", [1, NSB, 3, GQ])
    I["st_ffn_conv"] = din("st_ffn_conv", [DEPTH, NSB, 2, F2])
    wshapes = dict(
        norm_mix=[DEPTH, D], norm_ffn=[DEPTH, D], norm_final=[D],
        s5_w_in=[2, D, D], s5_a_re=[2, 64, 64], s5_a_im=[2, 64, 64], s5_log_dt=[2, 64],
        s5_b_re=[2, 64, 64, 16], s5_b_im=[2, 64, 64, 16], s5_c_re=[2, 64, 16, 64], s5_c_im=[2, 64, 16, 64],
        s5_d=[2, D], s5_w_glu=[2, D, 2 * D],
        lru_w_in=[1, D, 2 * LRU_W], lru_conv_w=[1, 4, LRU_W], lru_conv_b=[1, LRU_W],
        lru_w_gate_a=[1, 10, 128, 128], lru_b_gate_a=[1, LRU_W], lru_w_gate_x=[1, 10, 128, 128],
        lru_b_gate_x=[1, LRU_W], lru_lambda=[1, LRU_W], lru_w_out=[1, LRU_W, D],
        gdn_w_in=[1, D, 4112], gdn_conv_w=[1, 4, GQ], gdn_a_log=[1, 8], gdn_dt_bias=[1, 8],
        gdn_norm=[1, 128], gdn_w_out=[1, D, D],
        ffn_w_up=[DEPTH, D, F2], ffn_conv_w=[DEPTH, 3, F2], ffn_conv_b=[DEPTH, F2], ffn_w_down=[DEPTH, FH, D],
    )
    for n_, s_ in wshapes.items():
        I[n_] = din(n_, s_)
    O = {}
    O["yp"] = dout("yp", [NTP, D])
    O["ys"] = dout("ys", [NS, D])
    O["p_s5_re"] = dout("p_s5_re", [2, 64, 64])
    O["p_s5_im"] = dout("p_s5_im", [2, 64, 64])
    O["p_lru"] = dout("p_lru", [1, LRU_W])
    O["p_lru_conv"] = dout("p_lru_conv", [1, 3, LRU_W])
    O["p_gdn"] = dout("p_gdn", [1, 8, 128, 128])
    O["p_gdn_conv"] = dout("p_gdn_conv", [1, 3, GQ])
    O["p_ffn_conv"] = dout("p_ffn_conv", [DEPTH, 2, F2])
    O["s_s5_re"] = dout("s_s5_re", [2, NSB, 64, 64])
    O["s_s5_im"] = dout("s_s5_im", [2, NSB, 64, 64])
    O["s_lru"] = dout("s_lru", [1, NSB, LRU_W])
    O["s_lru_conv"] = dout("s_lru_conv", [1, NSB, 3, LRU_W])
    O["s_gdn"] = dout("s_gdn", [1, NSB, 8, 128, 128])
    O["s_gdn_conv"] = dout("s_gdn_conv", [1, NSB, 3, GQ])
    O["s_ffn_conv"] = dout("s_ffn_conv", [DEPTH, NSB, 2, F2])
    k.I, k.O = I, O

    k.XR = sb("XR", [128, DC, NT], F32)
    k.XN = sb("XN", [128, DC, NT], BF16)
    k.ident = sb("ident", [128, 128], F32)
    k.identb = sb("identb", [128, 128], BF16)
    k.onesb = sb("onesb", [128, 128], BF16)
    k.gam = sb("gam", [128, 2 * DEPTH + 1, DC], F32)
    k.ps = [st.enter_context(nc.psum_tensor(f"ps{i}", [128, 512], F32)) for i in range(8)]
    k.scr = sb("scr", [128, 2816], F32)
    k.stage = k.scr[:, 0:1024]
    k.stage2 = k.scr[:, 1024:2048]

    k.rstd = sb("rstd", [128, 512], F32)
    k.epsc = sb("epsc", [128, 1], F32)
    P.pe_embed = True
    _consts(k)
    if 's_consts' in dbg:
        run_prog(nc, P); st.close(); return nc
    _load_x(k)
    if 's_load' in dbg:
        run_prog(nc, P); st.close(); return nc
    if layers_cfg is None:
        layers_cfg = [("s5", 0), ("lru", 0), ("gdn", 0), ("s5", 1)][:n_depth]
    k.dbg = dbg
    if 'nolayers' in dbg:
        layers_cfg = []
    for li, (kind, j) in enumerate(layers_cfg):
        if kind is not None:
            _rmsnorm(k, gi=li)
            if kind == "s5":
                _s5(k, li, j)
            elif kind == "lru":
                _lru(k, li, j)
            elif kind == "gdn":
                P.pe_embed = False
                _gdn(k, li, j)
                P.pe_embed = True
        _rmsnorm(k, gi=DEPTH + li)
        if 'noffn' not in dbg:
            _ffn(k, li)
    _final(k)
    run_prog(nc, P)
    st.close()
    return nc


def _consts(k):
    P, nc = k.P, k.nc
    it = k.sb("iota_i", [128, 128], I32)
    P.add("pool", lambda e: e.iota(it[:], [[1, 128]], base=0, channel_multiplier=-1), writes=[it[:]])
    k.jmp = k.sb("jmp", [128, 128], F32)
    P.copy("dve", k.jmp[:], it[:])
    P.ts("dve", k.ident[:], k.jmp[:], 0.0, None, ALU.is_equal)
    P.copy("dve", k.identb[:], k.ident[:])
    P.memset("dve", k.onesb[:], 1.0)
    P.memset("dve", k.epsc[:], EPS)
    k.halfpi = k.sb("halfpi", [128, 1], F32)
    P.memset("dve", k.halfpi[:], float(np.pi / 2))
    k.onec = k.sb("onec", [128, 1], F32)
    P.memset("dve", k.onec[:], 1.0)
    rows = []
    for i in range(DEPTH):
        rows.append(k.I["norm_mix"][i:i + 1, :])
    for i in range(DEPTH):
        rows.append(k.I["norm_ffn"][i:i + 1, :])
    rows.append(k.I["norm_final"].rearrange("(o d) -> o d", o=1))
    load_cols(k, rows, D, k.gam)


def load_cols(k, rows, n, dst, eng="sp"):
    P = k.P
    R = len(rows)
    nchunk = n // 128
    stg = k.scr
    assert R <= 16 and n <= 2816
    for r, ap in enumerate(rows):
        P.dma(eng, stg[r:r + 1, 0:n], ap)
    done = 0
    while done < nchunk:
        m = min(nchunk - done, 512 // R)
        ps = k.ps[7]
        for c in range(m):
            P.tr(ps[:, c * R:(c + 1) * R], stg[0:R, (done + c) * 128:(done + c + 1) * 128], k.ident[0:R, 0:R])
        P.copy("dve", dst[:, :, done:done + m].rearrange("p r c -> p c r"),
               ps[:, 0:m * R].rearrange("p (c r) -> p c r", r=R))
        done += m


def _load_x(k):
    P = k.P
    srcs = [(k.I["xp"], i * 128, 128, i * 128) for i in range(NTP // 128)] + [(k.I["xs"], 0, NS, NTP)]
    for bi, (src, r0, n, c0) in enumerate(srcs):
        stg = k.stage if bi % 2 == 0 else k.stage2
        P.dma("sp", stg[0:n, :], src[r0:r0 + n, :])
        for half in range(2):
            ps = k.ps[(bi * 2 + half) % 4]
            for c in range(4):
                ch = half * 4 + c
                P.tr(ps[:, c * n:(c + 1) * n], stg[0:n, ch * 128:(ch + 1) * 128], k.ident[0:n, 0:n])
            eng = "dve" if half == 0 else "act"
            P.copy(eng, k.XR[:, half * 4:half * 4 + 4, c0:c0 + n],
                   ps[:, 0:4 * n].rearrange("p (c t) -> p c t", t=n))


def _rmsnorm(k, gi, out_f32=None):
    P = k.P
    k.ar_off = 0
    sqb = k.alloc([128, DC, 512], BF16)
    for ti, (c0, n, kind) in enumerate(TILES):
        sqv = sqb[:, :, 0:n]
        for c in range(DC):
            P.act(sqv[:, c, :], k.XR[:, c, c0:c0 + n], AF.Square)
        ps = k.ps[6 + (ti % 2)]
        for c in range(DC):
            P.mm(ps[:, 0:n], k.onesb[:], sqv[:, c, :], start=(c == 0), stop=(c == DC - 1))
        rs = k.rstd[:, 0:n]
        P.act(rs, ps[:, 0:n], AF.Sqrt, bias=k.epsc[:, 0:1], scale=1.0 / D)
        P.recip("dve", rs, rs)
        for c in range(DC):
            if out_f32 is None:
                P.stt("dve", k.XN[:, c, c0:c0 + n], k.XR[:, c, c0:c0 + n],
                      k.gam[:, gi, c:c + 1], rs, ALU.mult, ALU.mult)
            else:
                P.stt("dve", out_f32[:, c, 0:n], k.XR[:, c, c0:c0 + n],
                      k.gam[:, gi, c:c + 1], rs, ALU.mult, ALU.mult)
        if out_f32 is not None and 'nostore' not in k.dbg:
            if ('onlyp' in k.dbg and kind != 'p') or ('onlys' in k.dbg and kind != 's'):
                continue
            _store_y(k, out_f32, c0, n, kind)


def _store_y(k, yf, c0, n, kind):
    P = k.P
    dst = k.O["yp"] if kind == "p" else k.O["ys"]
    r0 = c0 if kind == "p" else 0
    nb = (n + 127) // 128
    for b in range(nb):
        m = min(128, n - b * 128)
        stg = k.stage if b % 2 == 0 else k.stage2
        for half in range(2):
            ps = k.ps[4 + half]
            for c in range(4):
                ch = half * 4 + c
                P.tr(ps[0:m, c * 128:(c + 1) * 128], yf[:, ch, b * 128:b * 128 + m], k.ident[:, :])
            P.copy("act" if half == 0 else "dve", stg[0:m, half * 512:(half + 1) * 512], ps[0:m, :])
        P.dma("sp", dst[r0 + b * 128:r0 + b * 128 + m, :], stg[0:m, :])


def _final(k):
    yf = k.alloc([128, DC, 512], F32, at=16 * 1024)
    _rmsnorm(k, gi=2 * DEPTH, out_f32=yf)


FFN_GROUPS = [list(range(i, min(i + 3, NPAIR))) for i in range(0, NPAIR, 3)]


def _ffn_setup(k):
    al = k.alloc
    k.ar_off = 8 * 1024
    k.WA = [al([128, 3, 3072], BF16) for i in range(2)]
    k.fcw = al([128, 4, 44], F32)
    k.fdiag = al([128, 3, 2, 3, 128], BF16)
    k.hext = al([128, 3, 2, 32 + 512], BF16)
    k.hexs = al([128, 3, 2, NSB * 6], BF16)
    k.ga = al([128, 2, 512], F32)
    k.G = al([128, 3, 512], BF16)
    k.ftail = al([128, 44, 2 + 2 * NSB], F32)
    k.fst = al([32, 768], F32)


def _ffn(k, li):
    P, I = k.P, k.I
    _ffn_setup(k)
    rows = [I["ffn_conv_w"][li, r:r + 1, :] for r in range(3)] + [I["ffn_conv_b"][li:li + 1, :]]
    for h in range(2):
        rr = [r_[:, h * FH:(h + 1) * FH] for r_ in rows]
        load_cols(k, rr, FH, k.fcw[:, :, h * 22:(h + 1) * 22])
    if 'ffn_a' in k.dbg:
        return
    for gi, grp in enumerate(FFN_GROUPS):
        if 'ffn_g1' in k.dbg and gi >= 1:
            break
        WA = k.WA[gi % 2]
        for s, p in enumerate(grp):
            for ab in range(2):
                col0 = ab * FH + p * 128
                src = I["ffn_w_up"][li][:, col0:col0 + 128].rearrange("(kc k) m -> k kc m", k=128)
                P.dma("pool", WA[:, s, ab * 1024:(ab + 1) * 1024].rearrange("k (kc m) -> k kc m", m=128), src)
            P.dma("pool", WA[:, s, 2048:3072], I["ffn_w_down"][li][p * 128:(p + 1) * 128, :])
        for s, p in enumerate(grp):
            for ab in range(2):
                col0 = ab * FH + p * 128
                P.dma("sp", k.fst[:, (s * 2 + ab) * 128:(s * 2 + ab + 1) * 128],
                      I["st_ffn_conv"][li].rearrange("b r f -> (b r) f")[:, col0:col0 + 128])
        for s, p in enumerate(grp):
            for ab in range(2):
                ch = ab * 22 + p
                for t in range(3):
                    P.ts("dve", k.fdiag[:, s, ab, t, :], k.ident[:], k.fcw[:, t, ch:ch + 1], None, ALU.mult)
                P.memset("dve", k.hext[:, s, ab, 30:32], 0.0)
                ps = k.ps[7]
                P.tr(ps[:, 0:32], k.fst[0:32, (s * 2 + ab) * 128:(s * 2 + ab + 1) * 128], k.ident[0:32, 0:32])
                P.copy("dve", k.hexs[:, s, ab, :].rearrange("p (b j) -> p b j", j=6)[:, :, 0:2],
                       ps[:, 0:32].rearrange("p (b r) -> p b r", r=2))
        if 'ffn_b' in k.dbg:
            continue
        for ti, (c0, n, kind) in enumerate(TILES):
            if ('onlyp' in k.dbg and kind != 'p') or ('onlys' in k.dbg and kind != 's'):
                continue
            if 'tile1' in k.dbg and ti >= 1:
                continue
            if 'tile2' in k.dbg and ti >= 2:
                continue
            last_p = (kind == "p" and c0 + n == NTP)

            def stage_u(s, p, set_):
                for ab in range(2):
                    ps = k.ps[set_ * 2 + ab]
                    for kc in range(DC):
                        P.mm(ps[:, 0:n], WA[:, s, ab * 1024 + kc * 128: ab * 1024 + (kc + 1) * 128],
                             k.XN[:, kc, c0:c0 + n], start=(kc == 0), stop=(kc == DC - 1))
                    ch = ab * 22 + p
                    if 'noevac' in k.dbg:
                        continue
                    if kind == "p":
                        P.copy("dve" if 'evdve' in k.dbg else "act", k.hext[:, s, ab, 32:32 + n], ps[:, 0:n])
                        if last_p:
                            P.copy("dve" if 'evdve' in k.dbg else "act", k.ftail[:, ch, 0:2], ps[:, n - 2:n])
                    else:
                        P.copy("dve", k.hexs[:, s, ab, :].rearrange("p (b j) -> p b j", j=6)[:, :, 2:6],
                               ps[:, 0:n].rearrange("p (b t) -> p b t", t=NST))
                        P.copy("dve", k.ftail[:, ch, 2:2 + 2 * NSB].rearrange("p (b r) -> p b r", r=2),
                               ps[:, 0:n].rearrange("p (b t) -> p b t", t=NST)[:, :, 2:4])

            def stage_c(s, p):
                for ab in range(2):
                    ps = k.ps[4 + ab]
                    for t in range(3):
                        if kind == "p":
                            rhs = k.hext[:, s, ab, 30 + t:30 + t + n]
                        else:
                            rhs = k.hexs[:, s, ab, :].rearrange("p (b j) -> p b j", j=6)[:, :, t:t + NST]
                        P.mm(ps[:, 0:n], k.fdiag[:, s, ab, t, :], rhs, start=(t == 0), stop=(t == 2))
                ga = k.ga[:, s % 2, 0:n]
                P.act(ga, k.ps[4][:, 0:n], AF.Gelu_apprx_tanh, bias=k.fcw[:, 3, p:p + 1])
                P.stt("dve", k.G[:, s, 0:n], k.ps[5][:, 0:n], k.fcw[:, 3, 22 + p:22 + p + 1], ga, ALU.add, ALU.mult)
                if kind == "p" and not last_p and 'nohalo' not in k.dbg:
                    for ab in range(2):
                        P.copy("pool", k.hext[:, s, ab, 30:32], k.hext[:, s, ab, 30 + n:32 + n])

            ns = len(grp)
            stage_u(0, grp[0], 0)
            for s in range(ns):
                if s + 1 < ns:
                    stage_u(s + 1, grp[s + 1], (s + 1) % 2)
                if 'nostc' not in k.dbg:
                    stage_c(s, grp[s])
            for oc in range(DC):
                if 'nodown' in k.dbg:
                    break
                ps = k.ps[6 + oc % 2]
                for s in range(ns):
                    P.mm(ps[:, 0:n], WA[:, s, 2048 + oc * 128:2048 + (oc + 1) * 128], k.G[:, s, 0:n],
                         start=(s == 0), stop=(s == ns - 1))
                P.tt("dve", k.XR[:, oc, c0:c0 + n], k.XR[:, oc, c0:c0 + n], ps[:, 0:n], ALU.add)
    if 'notail' in k.dbg:
        return
    _emit_tail(k, k.ftail, 44, k.O["p_ffn_conv"][li], k.O["s_ffn_conv"][li].rearrange("b r f -> (b r) f"), 2)


def _emit_tail(k, tail, nchunk, dst_p, dst_s, R):
    P = k.P
    W = R + R * NSB
    done = 0
    while done < nchunk:
        m = min(4, nchunk - done)
        ps = k.ps[7]
        for c in range(m):
            P.tr(ps[0:W, c * 128:(c + 1) * 128], tail[:, done + c, 0:W], k.ident[:, :])
        stg = k.stage
        P.copy("dve", stg[0:W, 0:m * 128], ps[0:W, 0:m * 128])
        P.dma("sp", dst_p[:, done * 128:(done + m) * 128], stg[0:R, 0:m * 128])
        P.dma("sp", dst_s[:, done * 128:(done + m) * 128], stg[R:W, 0:m * 128])
        done += m


def bc_last(ap, m):
    a = [list(x) for x in ap.ap]
    assert a[-1][1] == 1
    a[-1] = [0, m]
    return bass.AP(ap.tensor, ap.offset, a)


def ins_bc(ap, pos, m):
    a = [list(x) for x in ap.ap]
    a.insert(pos, [0, m])
    return bass.AP(ap.tensor, ap.offset, a)


S5_TC = 32
S5_TILES = [(i * 256, 256, "p") for i in range(8)] + [(NTP, NS, "s")]


def _s5(k, li, j):
    P, I, O, al = k.P, k.I, k.O, k.alloc
    TC = S5_TC
    k.ar_off = 8 * 1024
    mul, add, sub = ALU.mult, ALU.add, ALU.subtract
    LQ = al([128, 32, 2, 128], BF16)
    CT = al([128, 2, 8, 128], BF16)
    cosB = al([128, 32, TC], BF16)
    sinB = al([128, 32, TC], BF16)
    cl = al([128, 32], F32)
    sl = al([128, 32], F32)
    c4 = al([128, 32, 4], F32)
    s4 = al([128, 32, 4], F32)
    amul = al([128, 32, TC], F32)
    dd = al([128, 8, 128], BF16)
    h0 = [al([128, NSB, 32], F32) for _ in range(2)]
    hfin = h0
    hprev = [al([128, 32], F32) for _ in range(2)]
    er = al([128, 32], F32)
    sm = al([128, 12, 32], F32)
    dcl = al([128, 1, 8], F32)
    uoff = k.ar_off
    U = al([128, 8, 256], BF16)
    G = al([128, 8, 256], BF16)
    xoff = k.ar_off
    G1 = al([128, 32, TC], F32)
    G2 = al([128, 32, TC], F32)
    S1 = al([128, 32, TC], F32)
    S2 = al([128, 32, TC], F32)
    XBd = [al([128, 2, 32, TC], BF16) for _ in range(2)]
    TB1 = al([128, 32, TC], BF16)
    TB2 = al([128, 32, TC], BF16)
    TB3 = al([128, 32, TC], BF16)
    TB4 = al([128, 32, TC], BF16)
    HS = al([128, 2, 32, TC], BF16)
    xend = k.ar_off
    k.ar_off = xoff
    Wi = [al([128, 8, 128], BF16) for _ in range(8)]
    Wg = [al([128, 2, 8, 128], BF16) for _ in range(4)]
    SG = al([128, 256], F32)
    TM = al([128, 256], F32)
    k.ar_off = uoff
    Bn = [al([128, 32, 16], F32) for _ in range(2)]
    bb = [al([128, 32, 16], F32) for _ in range(2)]
    Bpad = al([128, 32, 32], F32)
    Z = al([128, 4, 128], F32)
    Cn = [al([128, 8, 64], F32) for _ in range(2)]
    Cpad = al([128, 8, 128], F32)
    rowt = al([128, 128], F32)
    mski = al([128, 2], I32)
    mskf = al([128, 4], F32)
    assert k.ar_off <= 93 * 1024, k.ar_off
    k.ar_off = max(k.ar_off, xend)

    def S(i):
        return sm[:, i, :]

    def load_T(rows_ap, R, dst):
        P.dma("sp", rowt[0:R, :], rows_ap)
        P.tr(k.ps[7][:, 0:R], rowt[0:R, :], k.ident[0:R, 0:R])
        P.copy("dve", dst, k.ps[7][:, 0:R])

    lre, lim, ldt = S(0), S(1), S(2)
    load_T(I["s5_a_re"][j].rearrange("(q g2) p -> q (g2 p)", g2=2), 32, lre)
    load_T(I["s5_a_im"][j].rearrange("(q g2) p -> q (g2 p)", g2=2), 32, lim)
    P.dma("sp", Z[0:32, 0, 0:2], I["s5_log_dt"][j].rearrange("(q g) -> q g", g=2))
    P.copy("dve", rowt[0:32, :].rearrange("q (g p) -> q g p", g=2),
           bc_last(Z[0:32, 0, 0:2].rearrange("q (g o) -> q g o", o=1), 64))
    P.tr(k.ps[7][:, 0:32], rowt[0:32, :], k.ident[0:32, 0:32])
    P.copy("dve", ldt, k.ps[7][:, 0:32])
    dt_ = S(2)
    P.act(dt_, ldt, AF.Exp)
    th = S(3)
    P.tt("dve", th, lim, dt_, mul)
    P.tt("dve", er, lre, dt_, mul)
    P.act(er, er, AF.Exp)
    cc, ss, t_a, t_b = S(4), S(5), S(6), S(7)
    P.act(ss, th, AF.Sin, scale=1.0 / 64)
    P.act(cc, th, AF.Sin, scale=1.0 / 64, bias=k.halfpi[:, 0:1])
    for _ in range(6):
        P.tt("dve", t_a, cc, cc, mul)
        P.tt("dve", t_b, ss, ss, mul)
        P.tt("dve", ss, cc, ss, mul)
        P.ts("dve", ss, ss, 2.0, None, mul)
        P.tt("dve", cc, t_a, t_b, sub)
    are, aim = S(6), S(7)
    P.tt("dve", are, er, cc, mul)
    P.tt("dve", aim, er, ss, mul)
    am1 = S(8)
    P.ts("dve", am1, are, -1.0, None, add)
    nr, ni, den = S(9), S(10), S(11)
    P.tt("dve", nr, am1, lre, mul)
    P.tt("dve", t_a_ := S(3), aim, lim, mul)
    P.tt("dve", nr, nr, S(3), add)
    P.tt("dve", ni, aim, lre, mul)
    P.tt("dve", S(3), am1, lim, mul)
    P.tt("dve", ni, ni, S(3), sub)
    P.tt("dve", den, lre, lre, mul)
    P.tt("dve", S(3), lim, lim, mul)
    P.tt("dve", den, den, S(3), add)
    P.recip("dve", den, den)
    kre, kim = S(9), S(10)
    P.tt("dve", kre, nr, den, mul)
    P.tt("dve", kim, ni, den, mul)
    cosT, sinT = G1, G2
    P.copy("dve", cosT[:, :, 0], cc)
    P.copy("dve", sinT[:, :, 0], ss)
    m = 1
    while m < TC:
        rc = bc_last(cosT[:, :, m - 1:m], m)
        rs = bc_last(sinT[:, :, m - 1:m], m)
        tA = S1[:, :, 0:m]
        tB = S2[:, :, 0:m]
        P.tt("dve", tA, cosT[:, :, 0:m], rc, mul)
        P.tt("dve", tB, sinT[:, :, 0:m], rs, mul)
        P.tt("dve", cosT[:, :, m:2 * m], tA, tB, sub)
        P.tt("dve", tA, sinT[:, :, 0:m], rc, mul)
        P.tt("dve", tB, cosT[:, :, 0:m], rs, mul)
        P.tt("dve", sinT[:, :, m:2 * m], tA, tB, add)
        m *= 2
    P.copy("dve", cosB[:, :, :], cosT[:, :, :])
    P.copy("dve", sinB[:, :, :], sinT[:, :, :])
    P.copy("dve", cl[:, :], cosT[:, :, TC - 1])
    P.copy("dve", sl[:, :], sinT[:, :, TC - 1])
    P.copy("dve", c4[:, :, :], cosT[:, :, 0:4])
    P.copy("dve", s4[:, :, :], sinT[:, :, 0:4])
    P.copy("dve", amul[:, :, :], bc_last(er.rearrange("p (q o) -> p q o", o=1), TC))
    P.memset("dve", amul[:, :, 0:1], 0.0)
    for ri, nm in enumerate(["s5_b_re", "s5_b_im"]):
        for g2 in range(2):
            src = I[nm][j].rearrange("(q g2) p c -> g2 p q c", g2=2)[g2]
            P.dma("sp", Bn[ri][64 * g2:64 * g2 + 64, :, :], src)
    kreb = bc_last(kre.rearrange("p (q o) -> p q o", o=1), 16)
    kimb = bc_last(kim.rearrange("p (q o) -> p q o", o=1), 16)
    tb = Bpad[:, :, 0:16]
    P.tt("dve", bb[0], Bn[0], kreb, mul)
    P.tt("dve", tb, Bn[1], kimb, mul)
    P.tt("dve", bb[0], bb[0], tb, sub)
    P.tt("dve", bb[1], Bn[1], kreb, mul)
    P.tt("dve", tb, Bn[0], kimb, mul)
    P.tt("dve", bb[1], bb[1], tb, add)
    P.memset("dve", Z[:], 0.0)
    for ri in range(2):
        P.memset("dve", Bpad[:], 0.0)
        P.copy("dve", Bpad[0:64, :, 0:16], bb[ri][0:64, :, :])
        P.copy("dve", Bpad[64:128, :, 16:32], bb[ri][64:128, :, :])
        for q in range(32):
            q4 = q % 4
            P.copy("dve", Z[:, q4, 32 * q4:32 * q4 + 32], Bpad[:, q, :])
            ps = k.ps[4 + q % 2]
            P.tr(ps[:, 0:128], Z[:, q4, :], k.ident[:])
            P.copy("act", LQ[:, q, ri, :], ps[:, 0:128])
    P.add("pool", lambda e: e.iota(mski[:, 0:1], [[0, 1]], base=0, channel_multiplier=1), writes=[mski[:, 0:1]])
    P.add("dve", lambda e: e.tensor_single_scalar(mski[:, 1:2], mski[:, 0:1], 4, ALU.arith_shift_right),
          reads=[mski[:, 0:1]], writes=[mski[:, 1:2]])
    P.add("dve", lambda e: e.tensor_single_scalar(mski[:, 0:1], mski[:, 1:2], 1, ALU.bitwise_and),
          reads=[mski[:, 1:2]], writes=[mski[:, 0:1]])
    P.copy("dve", mskf[:, 1:2], mski[:, 0:1])
    P.ts("dve", mskf[:, 0:1], mskf[:, 1:2], -1.0, 1.0, mul, add)
    P.ts("dve", mskf[:, 2:3], mskf[:, 0:1], -1.0, None, mul)
    P.ts("dve", mskf[:, 3:4], mskf[:, 1:2], -1.0, None, mul)
    for ri, nm in enumerate(["s5_c_re", "s5_c_im"]):
        P.dma("sp", Cn[ri][:, :, :], I[nm][j].rearrange("(ch g8) c p -> (g8 c) ch p", g8=8))
        P.ts("dve", Cpad[:, :, 0:64], Cn[ri][:, :, :], mskf[:, 2 * ri:2 * ri + 1], None, mul)
        P.ts("dve", Cpad[:, :, 64:128], Cn[ri][:, :, :], mskf[:, 2 * ri + 1:2 * ri + 2], None, mul)
        for ch in range(8):
            ps = k.ps[4 + ch % 2]
            P.tr(ps[:, 0:128], Cpad[:, ch, :], k.ident[:])
            P.copy("act", CT[:, ri, ch, :], ps[:, 0:128])
    load_cols(k, [I["s5_d"][j:j + 1, :]], D, dcl)
    for ch in range(8):
        P.ts("dve", dd[:, ch, :], k.ident[:], dcl[:, 0, ch:ch + 1], None, mul)
    for ri, nm in enumerate(["st_s5_re", "st_s5_im"]):
        rows = I[nm][j].rearrange("b (q g2) p -> (b q) (g2 p)", g2=2)
        for blk in range(4):
            P.dma("sp", rowt[:, :], rows[blk * 128:(blk + 1) * 128, :])
            ps = k.ps[6 + blk % 2]
            P.tr(ps[:, 0:128], rowt[:, :], k.ident[:])
            P.copy("dve", h0[ri][:, blk * 4:(blk + 1) * 4, :], ps[:, 0:128].rearrange("p (b q) -> p b q", q=32))
    for ri in range(2):
        P.memset("dve", hprev[ri][:], 0.0)

    sample_tables_done = False
    for ti, (c0, n, kind) in enumerate(S5_TILES):
        for mc in range(8):
            P.dma("pool", Wi[mc][:, :, :], I["s5_w_in"][j][:, mc * 128:(mc + 1) * 128].rearrange("(kc k) m -> k kc m", k=128))
        for mc in range(8):
            Wc = Wi[mc]
            ps = k.ps[4 + mc % 2]
            for kc in range(DC):
                P.mm(ps[:, 0:n], Wc[:, kc, :], k.XN[:, kc, c0:c0 + n], start=(kc == 0), stop=(kc == DC - 1))
            P.copy("act", U[:, mc, 0:n], ps[:, 0:n])
        if kind == "s" and not sample_tables_done:
            sample_tables_done = True
            for tab, t4 in ((cosB, c4), (sinB, s4)):
                P.copy("dve", tab[:, :, :].rearrange("p q (b t) -> p q b t", t=NST), ins_bc(t4[:, :, :], 2, TC // NST))
            P.copy("dve", amul[:, :, :], bc_last(er.rearrange("p (q o) -> p q o", o=1), TC))
            P.memset("dve", amul[:, :, :].rearrange("p q (b t) -> p q b t", t=NST)[:, :, :, 0:1], 0.0)
        nsub = n // TC
        for sc in range(nsub):
            u0 = sc * TC
            for ri in range(2):
                for q in range(32):
                    ps = k.ps[ri * 2 + q // 16]
                    P.mm(ps[:, (q % 16) * TC:(q % 16 + 1) * TC], LQ[:, q, ri, :], U[:, q // 4, u0:u0 + TC])
            XB = XBd[sc % 2]
            for ri in range(2):
                for half in range(2):
                    P.copy("act", XB[:, ri, half * 16:half * 16 + 16, :], k.ps[ri * 2 + half][:, :].rearrange("p (q t) -> p q t", t=TC))
            xr, xi = XB[:, 0, :, :], XB[:, 1, :, :]
            P.tt("dve", TB1[:, :, :], xr, cosB[:, :, :], mul)
            P.tt("dve", TB2[:, :, :], xi, sinB[:, :, :], mul)
            P.tt("dve", TB3[:, :, :], xi, cosB[:, :, :], mul)
            P.tt("dve", TB4[:, :, :], xr, sinB[:, :, :], mul)
            P.tt("dve", G1[:, :, :], TB1[:, :, :], TB2[:, :, :], add)
            P.tt("dve", G2[:, :, :], TB3[:, :, :], TB4[:, :, :], sub)
            if kind == "p":
                for ri, Gx in enumerate((G1, G2)):
                    P.tt("dve", S(ri), er, hprev[ri][:], mul)
                for ri, Gx in enumerate((G1, G2)):
                    P.tt("dve", Gx[:, :, 0], Gx[:, :, 0], S(ri), add)
            else:
                nb = TC // NST
                for ri, Gx in enumerate((G1, G2)):
                    tmp = S1[:, :, 0:nb]
                    hv = h0[ri][:, sc * nb:(sc + 1) * nb, :].rearrange("p b q -> p q b")
                    P.tt("dve", tmp, hv, bc_last(er.rearrange("p (q o) -> p q o", o=1), nb), mul)
                    gv = Gx[:, :, :].rearrange("p q (b t) -> p q b t", t=NST)[:, :, :, 0]
                    P.tt("dve", gv, gv, tmp, add)
            fl = lambda t_: t_[:, :, :].rearrange("p q t -> p (q t)")
            P.scan("dve", fl(S1), fl(amul), fl(G1), 0.0, mul, add)
            P.scan("dve", fl(S2), fl(amul), fl(G2), 0.0, mul, add)
            P.tt("dve", TB1[:, :, :], S1[:, :, :], cosB[:, :, :], mul)
            P.tt("dve", TB2[:, :, :], S2[:, :, :], sinB[:, :, :], mul)
            P.tt("dve", TB3[:, :, :], S2[:, :, :], cosB[:, :, :], mul)
            P.tt("dve", TB4[:, :, :], S1[:, :, :], sinB[:, :, :], mul)
            P.tt("dve", HS[:, 0, :, :], TB1[:, :, :], TB2[:, :, :], sub)
            P.tt("dve", HS[:, 1, :, :], TB3[:, :, :], TB4[:, :, :], add)
            if kind == "p":
                g1l, g2l = S1[:, :, TC - 1], S2[:, :, TC - 1]
                P.tt("pool", S(2), g1l, cl[:, :], mul)
                P.tt("pool", S(3), g2l, sl[:, :], mul)
                P.tt("pool", S(4), g2l, cl[:, :], mul)
                P.tt("pool", S(5), g1l, sl[:, :], mul)
                P.tt("pool", hprev[0][:], S(2), S(3), sub)
                P.tt("pool", hprev[1][:], S(4), S(5), add)
            else:
                nb = TC // NST
                v4 = lambda t_: t_[:, :, :].rearrange("p q (b t) -> p q b t", t=NST)[:, :, :, NST - 1]
                g1l, g2l = v4(S1), v4(S2)
                clb = bc_last(c4[:, :, NST - 1:NST], nb)
                slb = bc_last(s4[:, :, NST - 1:NST], nb)
                ta, tb_ = G1[:, :, 0:nb], G1[:, :, nb:2 * nb]
                of = [hfin[ri][:, sc * nb:(sc + 1) * nb, :].rearrange("p b q -> p q b") for ri in range(2)]
                P.tt("dve", ta, g1l, clb, mul)
                P.tt("dve", tb_, g2l, slb, mul)
                P.tt("dve", of[0], ta, tb_, sub)
                P.tt("dve", ta, g2l, clb, mul)
                P.tt("dve", tb_, g1l, slb, mul)
                P.tt("dve", of[1], ta, tb_, add)
            psy = k.ps[6 + sc % 2]
            for ch in range(8):
                ycols = slice(ch * TC, (ch + 1) * TC)
                P.mm(psy[:, ycols], dd[:, ch, :], U[:, ch, u0:u0 + TC], start=True, stop=False)
                for q4 in range(4):
                    for ri in range(2):
                        P.mm(psy[32 * q4:32 * q4 + 32, ycols], CT[:, ri, ch, 32 * q4:32 * q4 + 32], HS[:, ri, 4 * ch + q4, :],
                             start=False, stop=(ri == 1), tile_position=(0, 32 * q4))
            P.act(G[:, :, u0:u0 + TC], psy[:, 0:8 * TC].rearrange("p (c t) -> p c t", t=TC), AF.Gelu_apprx_tanh)
        def gload(oc_):
            for hv_ in range(2):
                col0 = hv_ * D + oc_ * 128
                P.dma("pool", Wg[oc_ % 4][:, hv_, :, :], I["s5_w_glu"][j][:, col0:col0 + 128].rearrange("(kc k) m -> k kc m", k=128))

        for oc in range(4):
            gload(oc)
        for oc in range(8):
            Wc = Wg[oc % 4]
            psv, psg = (k.ps[4], k.ps[5]) if oc % 2 == 0 else (k.ps[0], k.ps[1])
            for hv_, ps in ((0, psv), (1, psg)):
                for kc in range(DC):
                    P.mm(ps[:, 0:n], Wc[:, hv_, kc, :], G[:, kc, 0:n], start=(kc == 0), stop=(kc == DC - 1))
            if oc + 4 < 8:
                gload(oc + 4)
            P.act(SG[:, 0:n], psg[:, 0:n], AF.Sigmoid)
            P.tt("dve", TM[:, 0:n], psv[:, 0:n], SG[:, 0:n], mul)
            P.tt("dve", k.XR[:, oc, c0:c0 + n], k.XR[:, oc, c0:c0 + n], TM[:, 0:n], add)
    for ri, (pn, sn) in enumerate((("p_s5_re", "s_s5_re"), ("p_s5_im", "s_s5_im"))):
        ps = k.ps[4 + ri]
        P.tr(ps[0:32, 0:128], hprev[ri][:], k.ident[:])
        P.copy("dve", k.stage[0:32, 0:128], ps[0:32, 0:128])
        P.dma("sp", O[pn][j].rearrange("(q g2) p -> q (g2 p)", g2=2), k.stage[0:32, 0:128])
        rows = O[sn][j].rearrange("b (q g2) p -> (b q) (g2 p)", g2=2)
        for blk in range(4):
            ps2 = k.ps[6 + blk % 2]
            P.tr(ps2[:, 0:128], hfin[ri][:, blk * 4:(blk + 1) * 4, :].rearrange("p b q -> p (b q)"), k.ident[:])
            P.copy("dve", k.stage2[:, blk * 128:(blk + 1) * 128], ps2[:, 0:128])
            P.dma("sp", rows[blk * 128:(blk + 1) * 128, :], k.stage2[:, blk * 128:(blk + 1) * 128])


def _lru(k, li, j):
    P, I, al = k.P, k.I, k.alloc
    k.ar_off = 8 * 1024
    W = [al([128, 3328], BF16) for _ in range(2)]
    lcw = al([128, 8, 10], F32)
    lcc = al([128, 2, 10], F32)
    diag = al([128, 2, 4, 128], BF16)
    xext = al([128, 2, 32 + 512], BF16)
    xexs = al([128, 2, NSB * 7], BF16)
    xcb = al([128, 2, 512], BF16)
    xcf = al([128, 2, 512], F32)
    rr = al([128, 2, 512], F32)
    ig = al([128, 2, 512], F32)
    aa = al([128, 2, 512], F32)
    a2 = al([128, 2, 512], F32)
    hh = al([128, 2, 512], F32)
    gg = al([128, 2, 512], F32)
    Yb = al([128, 2, 512], BF16)
    hprev = al([128, 10], F32)
    h0 = al([128, 10, NSB], F32)
    ltail = al([128, 10, 3 + 3 * NSB], F32)
    lht = al([128, 10, 1 + NSB], F32)
    lst = al([48, 128], F32)
    rows = [I["lru_conv_w"][j, r:r + 1, :] for r in range(4)] + [I["lru_conv_b"][j:j + 1, :], I["lru_b_gate_a"][j:j + 1, :],
                                                                I["lru_b_gate_x"][j:j + 1, :], I["lru_lambda"][j:j + 1, :]]
    load_cols(k, rows, LRU_W, lcw)
    load_cols(k, [I["st_lru"][j, b:b + 1, :] for b in range(NSB)], LRU_W, h0.rearrange("p c b -> p b c"))
    P.act(lcc[:, 0, :], lcw[:, 7, :], AF.Exp, scale=-1.0)
    P.act(lcc[:, 0, :], lcc[:, 0, :], AF.Ln, bias=k.onec[:, 0:1])
    P.ts("dve", lcc[:, 1, :], lcc[:, 0, :], -16.0, None, ALU.mult)
    P.ts("dve", lcc[:, 0, :], lcc[:, 0, :], -8.0, None, ALU.mult)
    P.memset("dve", hprev[:], 0.0)

    def load_w(c):
        Wc = W[c % 2]
        for half in range(2):
            col0 = half * LRU_W + c * 128
            src = I["lru_w_in"][j][:, col0:col0 + 128].rearrange("(kc k) m -> k kc m", k=128)
            P.dma("pool", Wc[:, half * 1024:(half + 1) * 1024].rearrange("k (kc m) -> k kc m", m=128), src)
        P.dma("pool", Wc[:, 2048:2176], I["lru_w_gate_a"][j, c])
        P.dma("pool", Wc[:, 2176:2304], I["lru_w_gate_x"][j, c])
        P.dma("pool", Wc[:, 2304:3328], I["lru_w_out"][j][c * 128:(c + 1) * 128, :])

    load_w(0)
    for c in range(LRU_C):
        if c + 1 < LRU_C:
            load_w(c + 1)
        Wc = W[c % 2]
        d = c % 2
        for t in range(4):
            P.ts("dve", diag[:, d, t, :], k.ident[:], lcw[:, t, c:c + 1], None, ALU.mult)
        P.memset("dve", xext[:, d, 29:32], 0.0)
        P.dma("sp", lst[:, :], I["st_lru_conv"][j].rearrange("b r f -> (b r) f")[:, c * 128:(c + 1) * 128])
        P.tr(k.ps[7][:, 0:48], lst[0:48, :], k.ident[0:48, 0:48])
        P.copy("dve", xexs[:, d, :].rearrange("p (b q) -> p b q", q=7)[:, :, 0:3],
               k.ps[7][:, 0:48].rearrange("p (b r) -> p b r", r=3))
        def stage_a(ti, c0, n, kind):
            last_p = (kind == "p" and c0 + n == NTP)
            e = ti % 2
            psx, psg, psc, psr, psi = k.ps[0], k.ps[1], k.ps[2], k.ps[3], k.ps[4]
            for kc in range(DC):
                P.mm(psx[:, 0:n], Wc[:, 1024 + kc * 128:1024 + (kc + 1) * 128], k.XN[:, kc, c0:c0 + n],
                     start=(kc == 0), stop=(kc == DC - 1))
            for kc in range(DC):
                P.mm(psg[:, 0:n], Wc[:, kc * 128:(kc + 1) * 128], k.XN[:, kc, c0:c0 + n],
                     start=(kc == 0), stop=(kc == DC - 1))
            if kind == "p":
                P.copy("act", xext[:, d, 32:32 + n], psx[:, 0:n])
                if last_p:
                    P.copy("act", ltail[:, c, 0:3], psx[:, n - 3:n])
                taps = [xext[:, d, 29 + t:29 + t + n] for t in range(4)]
            else:
                xv = xexs[:, d, :].rearrange("p (b q) -> p b q", q=7)
                pv = psx[:, 0:n].rearrange("p (b t) -> p b t", t=NST)
                P.copy("dve", xv[:, :, 3:7], pv)
                P.copy("dve", ltail[:, c, 3:3 + 3 * NSB].rearrange("p (b r) -> p b r", r=3), pv[:, :, 1:4])
                taps = [xv[:, :, t:t + NST] for t in range(4)]
            P.act(gg[:, e, 0:n], psg[:, 0:n], AF.Gelu_apprx_tanh)
            for t in range(4):
                P.mm(psc[:, 0:n], diag[:, d, t, :], taps[t], start=(t == 0), stop=(t == 3))
            if kind == "p" and not last_p:
                P.copy("pool", xext[:, d, 29:32], xext[:, d, 29 + n:32 + n])
            P.act(xcb[:, e, 0:n], psc[:, 0:n], AF.Identity, bias=lcw[:, 4, c:c + 1])
            P.act(xcf[:, e, 0:n], psc[:, 0:n], AF.Identity, bias=lcw[:, 4, c:c + 1])
            P.mm(psr[:, 0:n], Wc[:, 2048:2176], xcb[:, e, 0:n])
            P.mm(psi[:, 0:n], Wc[:, 2176:2304], xcb[:, e, 0:n])
            P.act(rr[:, e, 0:n], psr[:, 0:n], AF.Sigmoid, bias=lcw[:, 5, c:c + 1])
            P.act(ig[:, e, 0:n], psi[:, 0:n], AF.Sigmoid, bias=lcw[:, 6, c:c + 1])

        def stage_b(ti, c0, n, kind):
            last_p = (kind == "p" and c0 + n == NTP)
            e = ti % 2
            P.act(aa[:, e, 0:n], rr[:, e, 0:n], AF.Exp, scale=lcc[:, 0, c:c + 1])
            P.act(a2[:, e, 0:n], rr[:, e, 0:n], AF.Exp, scale=lcc[:, 1, c:c + 1])
            P.act(a2[:, e, 0:n], a2[:, e, 0:n], AF.Sqrt, bias=k.onec[:, 0:1], scale=-1.0)
            P.tt("dve", ig[:, e, 0:n], ig[:, e, 0:n], xcf[:, e, 0:n], ALU.mult)
            P.tt("dve", ig[:, e, 0:n], ig[:, e, 0:n], a2[:, e, 0:n], ALU.mult)
            if kind == "p":
                P.scan("dve", hh[:, e, 0:n], aa[:, e, 0:n], ig[:, e, 0:n], hprev[:, c:c + 1], ALU.mult, ALU.add)
                P.copy("dve", hprev[:, c:c + 1], hh[:, e, n - 1:n])
                if last_p:
                    P.copy("dve", lht[:, c, 0:1], hh[:, e, n - 1:n])
            else:
                av = aa[:, e, 0:n].rearrange("p (b t) -> p b t", t=NST)
                bv = ig[:, e, 0:n].rearrange("p (b t) -> p b t", t=NST)
                tmp = rr[:, e, 0:NSB]
                P.tt("dve", tmp, av[:, :, 0], h0[:, c, :], ALU.mult)
                P.tt("dve", bv[:, :, 0], bv[:, :, 0], tmp, ALU.add)
                P.memset("dve", av[:, :, 0], 0.0)
                P.scan("dve", hh[:, e, 0:n], aa[:, e, 0:n], ig[:, e, 0:n], 0.0, ALU.mult, ALU.add)
                P.copy("dve", lht[:, c, 1:1 + NSB], hh[:, e, 0:n].rearrange("p (b t) -> p b t", t=NST)[:, :, NST - 1])
            P.tt("dve", Yb[:, e, 0:n], gg[:, e, 0:n], hh[:, e, 0:n], ALU.mult)
            for oc in range(DC):
                ps = k.ps[5 + oc % 3]
                P.mm(ps[:, 0:n], Wc[:, 2304 + oc * 128:2304 + (oc + 1) * 128], Yb[:, e, 0:n])
                P.tt("dve", k.XR[:, oc, c0:c0 + n], k.XR[:, oc, c0:c0 + n], ps[:, 0:n], ALU.add)

        stage_a(0, *TILES[0])
        for ti in range(len(TILES)):
            if ti + 1 < len(TILES):
                stage_a(ti + 1, *TILES[ti + 1])
            stage_b(ti, *TILES[ti])
    _emit_tail(k, ltail, 10, k.O["p_lru_conv"][j], k.O["s_lru_conv"][j].rearrange("b r f -> (b r) f"), 3)
    _emit_tail(k, lht, 10, k.O["p_lru"][j:j + 1, :], k.O["s_lru"][j], 1)


GDN_TILES = [(i * 256, 256, "p") for i in range(8)] + [(NTP, NS, "s")]


def _gdn(k, li, j):
    P, I, O, al = k.P, k.I, k.O, k.alloc
    mul, add, sub = ALU.mult, ALU.add, ALU.subtract
    k.ar_off = 8 * 1024
    TN = 256
    UT = al([128, 128], F32)
    MT = al([128, 128], F32)
    MM = al([128, 128], F32)
    MTs = al([64, 64], F32)
    MMs = al([64, 64], F32)
    UTs = al([64, 64], F32)
    colm = al([128, NSB, 64], BF16)
    rowm = al([64, NSB], F32)
    lastm = al([64, 64], F32)
    cst = al([128, 2, 8], F32)
    nw = al([128, 1], F32)
    nwt = al([128, 1, 1], F32)
    gcw = al([128, 4, 24], F32)
    halo = al([128, 24, 3], BF16)
    gtail = al([128, 24, 3 + 3 * NSB], F32)
    onesf = al([128, 128], F32)
    ti_ = al([128, 128], I32)
    tf_ = al([128, 128], F32)
    tp_ = al([128, 2], F32)
    tpi = al([128, 2], I32)
    QK = al([128, 16, TN], BF16)
    V = al([128, 8, TN], BF16)
    Zs = al([128, 8, TN], BF16)
    AB = al([16, TN], F32)
    OT = al([128, 8, TN], BF16)
    S = al([128, 8, 128], F32)
    Sb = al([128, 8, 128], BF16)
    woff = k.ar_off
    Wp = [al([128, 8, 128], BF16) for _ in range(8)]
    Wab = al([128, 8, 16], BF16)
    dg = al([128, 2, 4, 128], BF16)
    xext = al([128, 2, 32 + TN], BF16)
    xexs = al([128, 2, NSB * 7], BF16)
    raw = al([128, 2, TN], BF16)
    sq = al([128, 2, TN], BF16)
    rs = al([128, 2, TN], F32)
    lst = al([48, 128], F32)
    k.ar_off = woff
    Dg = al([128, 4, 128], F32)
    Wo = [Dg.rearrange("p h c -> p (h c)").bitcast(BF16)[:, i * 1024:(i + 1) * 1024].rearrange("p (a b) -> p a b", b=128)
          for i in range(1)]
    ab = al([128, 16], F32)
    sm = al([128, 12, 8], F32)
    egl = al([128, 8, NSB], F32)
    DT = al([128, 4, 128], F32)
    Dm = al([128, 4, 128], F32)
    ktok = al([128, 4, 128], BF16)
    vtok = al([128, 4, 128], BF16)
    Nn = [al([128, 4, 128], F32) for _ in range(2)]
    NT = [al([128, 4, 128], F32) for _ in range(2)]
    X = [al([128, 4, 256], F32) for _ in range(2)]
    wT = al([128, 4, 128], BF16)
    vn = al([128, 4, 128], F32)
    vnb = al([128, 4, 128], BF16)
    aT = al([128, 4, 128], BF16)
    tq = al([128, 4, 128], F32)
    kd = al([128, 4, 128], BF16)
    o = al([128, 8, 128], F32)
    msk = al([128, NSB, 64], BF16)
    kdm = al([64, NSB, 128], BF16)
    Wo.append(al([128, 8, 128], BF16))
    Ssb = S.rearrange("p h d -> p (h d)").bitcast(BF16).rearrange("p (b d) -> p b d", d=128)
    Ss = Sb.rearrange("p h d -> p (h d)").bitcast(F32).rearrange("p (b d) -> p b d", d=128)
    assert k.ar_off <= 93 * 1024, k.ar_off
    BETA, NBETA, GG, CUM, NCUM, ECUM, BEC, DL, EDL, SS, RSTD, TMP = range(12)

    P.ts("dve", UT[:], k.jmp[:], 0.0, None, ALU.is_ge)
    P.ts("dve", MT[:], UT[:], -1.0, 1e4, add, mul)
    P.ts("dve", MM[:], UT[:], 1e4, None, mul)
    P.memset("dve", onesf[:], 1.0)
    P.add("pool", lambda e: e.iota(ti_[:], [[1, 128]], base=0, channel_multiplier=0), writes=[ti_[:]])
    P.add("dve", lambda e: e.tensor_single_scalar(ti_[:], ti_[:], 2, ALU.arith_shift_right), reads=[ti_[:]], writes=[ti_[:]])
    P.copy("dve", tf_[:], ti_[:])
    P.add("pool", lambda e: e.iota(tpi[:, 0:1], [[0, 1]], base=0, channel_multiplier=1), writes=[tpi[:, 0:1]])
    P.add("dve", lambda e: e.tensor_single_scalar(tpi[:, 1:2], tpi[:, 0:1], 2, ALU.arith_shift_right),
          reads=[tpi[:, 0:1]], writes=[tpi[:, 1:2]])
    P.copy("dve", tp_[:, 0:1], tpi[:, 1:2])
    same = Dg[0:64, 0, 0:64]
    P.ts("dve", same, tf_[0:64, 0:64], tp_[0:64, 0:1], None, ALU.is_equal)
    P.tt("dve", UTs[:], UT[0:64, 0:64], same, mul)
    P.ts("dve", MTs[:], UTs[:], -1.0, 1e4, add, mul)
    vld = Dg[0:64, 1, 0:64]
    P.ts("dve", vld, UT[0:64, 0:64], -1.0, 1.0, mul, add)
    P.tt("dve", vld, vld, same, mul)
    P.ts("dve", MMs[:], vld, -1.0, 1.0, mul, add)
    P.ts("dve", MMs[:], MMs[:], 1e4, None, mul)
    for b in range(NSB):
        P.ts("dve", colm[:, b, :], tf_[:, 0:64], float(b), None, ALU.is_equal)
    bidx = Dg[0:64, 2, 0:NSB]
    P.add("pool", lambda e: e.iota(ti_[0:64, 0:NSB], [[1, NSB]], base=0, channel_multiplier=0), writes=[ti_[0:64, 0:NSB]])
    P.copy("dve", bidx, ti_[0:64, 0:NSB])
    P.ts("dve", rowm[:], bidx, tp_[0:64, 0:1], None, ALU.is_equal)
    P.ts("dve", tp_[:, 1:2], tp_[:, 0:1], 4.0, 3.0, mul, add)
    P.add("pool", lambda e: e.iota(ti_[0:64, 0:64], [[1, 64]], base=0, channel_multiplier=0), writes=[ti_[0:64, 0:64]])
    P.copy("dve", Dg[0:64, 3, 0:64], ti_[0:64, 0:64])
    P.ts("dve", lastm[:], Dg[0:64, 3, 0:64], tp_[0:64, 1:2], None, ALU.is_equal)
    alog = I["gdn_a_log"][j]
    dtb = I["gdn_dt_bias"][j]
    P.dma("sp", cst[:, 0, :], bass.AP(alog.tensor, alog.offset, [[0, 128], [1, 8]]))
    P.dma("sp", cst[:, 1, :], bass.AP(dtb.tensor, dtb.offset, [[0, 128], [1, 8]]))
    P.act(cst[:, 0, :], cst[:, 0, :], AF.Exp)
    P.ts("dve", cst[:, 0, :], cst[:, 0, :], -1.0, None, mul)
    load_cols(k, [I["gdn_norm"][j:j + 1, :]], 128, nwt)
    P.copy("dve", nw[:], nwt[:, 0, :])
    load_cols(k, [I["gdn_conv_w"][j, r:r + 1, 0:2816] for r in range(4)], 2816, gcw[:, :, 0:22])
    load_cols(k, [I["gdn_conv_w"][j, r:r + 1, 2816:3072] for r in range(4)], 256, gcw[:, :, 22:24])
    P.memset("dve", halo[:], 0.0)
    P.memset("dve", S[:], 0.0)
    P.memset("dve", Sb[:], 0.0)

    def proj_tile(c0, n, kind):
        last_p = (kind == "p" and c0 + n == NTP)
        P.dma("pool", Wab[:, :, :], I["gdn_w_in"][j][:, 4096:4112].rearrange("(kc k) m -> k kc m", k=128))
        for kc in range(DC):
            P.mm(k.ps[3][0:16, 0:n], Wab[:, kc, :], k.XN[:, kc, c0:c0 + n], start=(kc == 0), stop=(kc == DC - 1))
        P.copy("dve", AB[:, 0:n], k.ps[3][0:16, 0:n])
        taps_of = {}

        def wload(ch):
            P.dma("pool", Wp[ch % 8][:, :, :], I["gdn_w_in"][j][:, ch * 128:(ch + 1) * 128].rearrange("(kc k) m -> k kc m", k=128))

        for ch0 in range(8):
            wload(ch0)

        def s1(ch):
            Wc = Wp[ch % 8]
            d = ch % 2
            psx = k.ps[ch % 2]
            for kc in range(DC):
                P.mm(psx[:, 0:n], Wc[:, kc, :], k.XN[:, kc, c0:c0 + n], start=(kc == 0), stop=(kc == DC - 1))
            if ch + 8 < 32:
                wload(ch + 8)
            if ch >= 24:
                P.act(Zs[:, ch - 24, 0:n], psx[:, 0:n], AF.Silu)
                return
            for t in range(4):
                P.ts("dve", dg[:, d, t, :], k.ident[:], gcw[:, t, ch:ch + 1], None, mul)
            if kind == "p":
                P.copy("pool", xext[:, d, 29:32], halo[:, ch, :])
                P.copy("act", xext[:, d, 32:32 + n], psx[:, 0:n])
                if last_p:
                    P.copy("act", gtail[:, ch, 0:3], psx[:, n - 3:n])
                else:
                    P.copy("pool", halo[:, ch, :], xext[:, d, 29 + n:32 + n])
                taps_of[ch] = [xext[:, d, 29 + t:29 + t + n] for t in range(4)]
            else:
                P.dma("sp", lst[:, :], I["st_gdn_conv"][j].rearrange("b r f -> (b r) f")[:, ch * 128:(ch + 1) * 128])
                P.tr(k.ps[7][:, 0:48], lst[0:48, :], k.ident[0:48, 0:48])
                xv = xexs[:, d, :].rearrange("p (b q) -> p b q", q=7)
                P.copy("dve", xv[:, :, 0:3], k.ps[7][:, 0:48].rearrange("p (b r) -> p b r", r=3))
                pv = psx[:, 0:n].rearrange("p (b t) -> p b t", t=NST)
                P.copy("dve", xv[:, :, 3:7], pv)
                P.copy("dve", gtail[:, ch, 3:3 + 3 * NSB].rearrange("p (b r) -> p b r", r=3), pv[:, :, 1:4])
                taps_of[ch] = [xv[:, :, t:t + NST] for t in range(4)]

        def s2(ch):
            if ch >= 24:
                return
            d = ch % 2
            psc = k.ps[2 if ch % 2 == 0 else 5]
            taps = taps_of[ch]
            for t in range(4):
                P.mm(psc[:, 0:n], dg[:, d, t, :], taps[t], start=(t == 0), stop=(t == 3))
            if ch >= 16:
                P.act(V[:, ch - 16, 0:n], psc[:, 0:n], AF.Silu)
                return
            P.act(raw[:, d, 0:n], psc[:, 0:n], AF.Silu)
            P.act(sq[:, d, 0:n], raw[:, d, 0:n], AF.Square)
            pss = k.ps[3 if ch % 2 == 0 else 6]
            P.mm(pss[:, 0:n], k.onesb[:], sq[:, d, 0:n])

        def s3(ch):
            if ch >= 16:
                return
            d = ch % 2
            pss = k.ps[3 if ch % 2 == 0 else 6]
            P.act(rs[:, d, 0:n], pss[:, 0:n], AF.Sqrt, bias=k.epsc[:, 0:1])
            P.recip("dve", rs[:, d, 0:n], rs[:, d, 0:n])
            if ch < 8:
                P.stt("dve", QK[:, ch, 0:n], raw[:, d, 0:n], float(128 ** -0.5), rs[:, d, 0:n], mul, mul)
            else:
                P.tt("dve", QK[:, ch, 0:n], raw[:, d, 0:n], rs[:, d, 0:n], mul)

        NCH = 32
        for it in range(NCH + 2):
            if it < NCH:
                s1(it)
            if 0 <= it - 1 < NCH:
                s2(it - 1)
            if 0 <= it - 2 < NCH:
                s3(it - 2)

    def chunk(t0, C, nseq):
        cols = slice(t0, t0 + C)
        sample = nseq > 1
        idC = k.ident[0:C, 0:C]
        sv = lambda i: sm[0:C, i, :]
        P.tr(k.ps[7][0:C, 0:16], AB[0:16, cols], k.ident[0:16, 0:16])
        P.copy("dve", ab[0:C, :], k.ps[7][0:C, 0:16])
        P.act(sv(BETA), ab[0:C, 8:16], AF.Sigmoid)
        P.ts("dve", sv(NBETA), sv(BETA), -1.0, None, mul)
        P.tt("dve", sv(TMP), ab[0:C, 0:8], cst[0:C, 1, :], add)
        P.act(sv(TMP), sv(TMP), AF.Exp)
        P.act(sv(TMP), sv(TMP), AF.Ln, bias=k.onec[0:C, 0:1])
        P.tt("dve", sv(GG), sv(TMP), cst[0:C, 0, :], mul)
        P.mm(k.ps[6][0:C, 0:8], (UTs[:, :] if sample else UT[:, :]), sv(GG))
        P.copy("dve", sv(CUM), k.ps[6][0:C, 0:8])
        P.ts("dve", sv(NCUM), sv(CUM), -1.0, None, mul)
        P.act(sv(ECUM), sv(CUM), AF.Exp)
        P.tt("dve", sv(BEC), sv(BETA), sv(ECUM), mul)
        mt = MTs[:, :] if sample else MT[:, :]
        mm_ = MMs[:, :] if sample else MM[:, :]
        for hg in range(2):
            hs = [hg * 4 + i for i in range(4)]
            psA, psB, psC, psD, psR = k.ps[0], k.ps[1], k.ps[2], k.ps[3], k.ps[4]
            P.tt("dve", Dg[0:C, :, 0:C], ins_bc(idC, 1, 4),
                 bc_last(sm[0:C, CUM, hg * 4:hg * 4 + 4].rearrange("p (h o) -> p h o", o=1), C), mul)
            P.mm(psR[:, 0:4 * C].rearrange("p (h c) -> p h c", c=C), onesf[0:C, :], Dg[0:C, :, 0:C])

            def Rv(i, rows=C):
                return psR[0:rows, i * C:(i + 1) * C]

            for i, h in enumerate(hs):
                P.tt("dve", DT[0:C, i, 0:C], Rv(i), mt, add)
                P.tt("dve", Dm[0:C, i, 0:C], Rv(i), mm_, add)
            for i, h in enumerate(hs):
                P.act(DT[0:C, i, 0:C], DT[0:C, i, 0:C], AF.Exp, bias=sm[0:C, NCUM, h:h + 1])
                P.act(Dm[0:C, i, 0:C], Dm[0:C, i, 0:C], AF.Exp, bias=sm[0:C, CUM, h:h + 1], scale=-1.0)
            if not sample:
                for i, h in enumerate(hs):
                    P.tt("dve", sm[0:C, DL, h:h + 1], Rv(i)[:, C - 1:C], sm[0:C, CUM, h:h + 1], sub)
                    P.act(egl[:, h, 0:1], Rv(i, 128)[:, C - 1:C], AF.Exp)
            else:
                for i, h in enumerate(hs):
                    P.tt("dve", Dg[0:C, i, 0:C], Rv(i), lastm[:, :], mul)
                    P.add("dve", lambda e, i=i, h=h: e.reduce_sum(sm[0:C, TMP, h:h + 1], Dg[0:C, i, 0:C], AX.X),
                          reads=[Dg[0:C, i, 0:C]], writes=[sm[0:C, TMP, h:h + 1]])
                    P.act(egl[:, h, :], Rv(i, 128).rearrange("p (b t) -> p b t", t=NST)[:, :, NST - 1], AF.Exp)
                P.tt("dve", sm[0:C, DL, hg * 4:hg * 4 + 4], sm[0:C, TMP, hg * 4:hg * 4 + 4], sm[0:C, CUM, hg * 4:hg * 4 + 4], sub)
            P.act(sm[0:C, EDL, hg * 4:hg * 4 + 4], sm[0:C, DL, hg * 4:hg * 4 + 4], AF.Exp)
            pT = psA[:, :].bitcast(BF16)
            for i, h in enumerate(hs):
                P.tr(pT[0:C, i * 128:(i + 1) * 128], QK[:, 8 + h, cols], k.identb[:])
                P.tr(pT[0:C, 512 + i * 128:512 + (i + 1) * 128], V[:, h, cols], k.identb[:])
            P.copy("act", ktok[0:C, :, :], pT[0:C, 0:512].rearrange("p (h d) -> p h d", d=128))
            P.copy("act", vtok[0:C, :, :], pT[0:C, 512:1024].rearrange("p (h d) -> p h d", d=128))
            for i, h in enumerate(hs):
                P.mm(psB[0:C, i * C:(i + 1) * C], QK[:, 8 + h, cols], QK[:, 8 + h, cols])
            for i, h in enumerate(hs):
                P.stt("dve", Nn[0][0:C, i, 0:C], psB[0:C, i * C:(i + 1) * C], sm[0:C, NBETA, h:h + 1], Dm[0:C, i, 0:C], mul, mul)
            for i, h in enumerate(hs):
                P.tr(psC[0:C, i * C:(i + 1) * C], Nn[0][0:C, i, 0:C], idC)
            P.copy("act", NT[0][0:C, :, 0:C], psC[0:C, 0:4 * C].rearrange("p (h c) -> p h c", c=C))
            for i, h in enumerate(hs):
                P.ts("dve", X[0][0:C, i, 0:128], vtok[0:C, i, :], sm[0:C, BETA, h:h + 1], None, mul)
                P.ts("dve", X[0][0:C, i, 128:256], ktok[0:C, i, :], sm[0:C, BEC, h:h + 1], None, mul)
            nlev = 2 if sample else 7
            cur = 0
            for lev in range(nlev):
                for i in range(4):
                    P.mm(k.ps[2 + i // 2][0:C, (i % 2) * 256:(i % 2 + 1) * 256], NT[cur][0:C, i, 0:C], X[lev % 2][0:C, i, :])
                for half in range(2):
                    P.tt("dve", X[(lev + 1) % 2][0:C, 2 * half:2 * half + 2, :], X[lev % 2][0:C, 2 * half:2 * half + 2, :],
                         k.ps[2 + half][0:C, 0:512].rearrange("p (h d) -> p h d", d=256), add)
                if lev < nlev - 1:
                    for i in range(4):
                        P.mm(psA[0:C, i * C:(i + 1) * C], NT[cur][0:C, i, 0:C], Nn[cur][0:C, i, 0:C])
                        P.mm(psB[0:C, i * C:(i + 1) * C], Nn[cur][0:C, i, 0:C], NT[cur][0:C, i, 0:C])
                    P.copy("act", Nn[1 - cur][0:C, :, 0:C], psA[0:C, 0:4 * C].rearrange("p (h c) -> p h c", c=C))
                    P.copy("act", NT[1 - cur][0:C, :, 0:C], psB[0:C, 0:4 * C].rearrange("p (h c) -> p h c", c=C))
                    cur = 1 - cur
            Xs = X[nlev % 2]
            for i in range(4):
                P.tr(psA[:, i * C:(i + 1) * C], Xs[0:C, i, 128:256], idC)
            P.copy("act", wT[:, :, 0:C], psA[:, 0:4 * C].rearrange("p (h c) -> p h c", c=C))
            if not sample:
                for i, h in enumerate(hs):
                    P.mm(psB[0:C, i * 128:(i + 1) * 128], wT[:, i, 0:C], Sb[:, h, :])
                    P.mm(psC[0:C, i * 128:(i + 1) * 128], QK[:, h, cols], Sb[:, h, :])
                sample_ws = None
            else:
                for i, h in enumerate(hs):
                    P.dma("pool", Ssb[:, :, :], I["st_gdn"][j][:, h, :, :].rearrange("b k v -> k b v"))
                    P.tt("dve", msk[:, :, :], ins_bc(wT[:, i, 0:C], 1, NSB), colm[:, :, :], mul)
                    for b in range(NSB):
                        P.mm(psB[0:C, i * 128:(i + 1) * 128], msk[:, b, :], Ssb[:, b, :], start=(b == 0), stop=(b == NSB - 1))
                    P.tt("dve", msk[:, :, :], ins_bc(QK[:, h, cols], 1, NSB), colm[:, :, :], mul)
                    for b in range(NSB):
                        P.mm(psC[0:C, i * 128:(i + 1) * 128], msk[:, b, :], Ssb[:, b, :], start=(b == 0), stop=(b == NSB - 1))
            P.tt("dve", vn[0:C, :, :], Xs[0:C, :, 0:128], psB[0:C, 0:512].rearrange("p (h d) -> p h d", d=128), sub)
            P.copy("act", vnb[0:C, :, :], vn[0:C, :, :])
            for i, h in enumerate(hs):
                P.act(tq[0:C, i, :], psC[0:C, i * 128:(i + 1) * 128], AF.Identity, scale=sm[0:C, ECUM, h:h + 1])
            for i, h in enumerate(hs):
                P.mm(psD[0:C, i * C:(i + 1) * C], QK[:, 8 + h, cols], QK[:, h, cols])
            P.tt("dve", aT[0:C, :, 0:C], psD[0:C, 0:4 * C].rearrange("p (h c) -> p h c", c=C), DT[0:C, :, 0:C], mul)
            for i, h in enumerate(hs):
                P.mm(psB[0:C, i * 128:(i + 1) * 128], aT[0:C, i, 0:C], vnb[0:C, i, :])
            P.tt("dve", o[0:C, hg * 4:hg * 4 + 4, :], tq[0:C, :, :], psB[0:C, 0:512].rearrange("p (h d) -> p h d", d=128), add)
            P.tt("dve", kd[0:C, :, :], ktok[0:C, :, :],
                 bc_last(sm[0:C, EDL, hg * 4:hg * 4 + 4].rearrange("p (h o) -> p h o", o=1), 128), mul)
            if not sample:
                for i, h in enumerate(hs):
                    P.mm(psC[:, i * 128:(i + 1) * 128], kd[0:C, i, :], vnb[0:C, i, :])
                for i, h in enumerate(hs):
                    P.stt("dve", S[:, h, :], S[:, h, :], egl[:, h, 0:1], psC[:, i * 128:(i + 1) * 128], mul, add)
                P.copy("act", Sb[:, hg * 4:hg * 4 + 4, :], S[:, hg * 4:hg * 4 + 4, :])
            else:
                for i, h in enumerate(hs):
                    P.tt("dve", kdm[:, :, :], ins_bc(kd[0:C, i, :], 1, NSB),
                         bc_last(rowm[:, :].rearrange("p (b o) -> p b o", o=1), 128), mul)
                    for b4 in range(NSB // 4):
                        P.dma("sp", Ss[:, :, :], I["st_gdn"][j][b4 * 4:(b4 + 1) * 4, h, :, :].rearrange("b k v -> k b v"))
                        for bb in range(4):
                            P.mm(psC[:, bb * 128:(bb + 1) * 128], kdm[:, b4 * 4 + bb, :], vnb[0:C, i, :])
                        for bb in range(4):
                            b = b4 * 4 + bb
                            P.stt("dve", Ss[:, bb, :], Ss[:, bb, :], egl[:, h, b:b + 1], psC[:, bb * 128:(bb + 1) * 128], mul, add)
                        P.dma("sp", O["s_gdn"][j][b4 * 4:(b4 + 1) * 4, h, :, :].rearrange("b k v -> k b v"), Ss[:, :, :])
        for h in range(8):
            P.add("act", lambda e, h=h: e.activation(X[0][0:C, h % 4, 0:128], o[0:C, h, :], AF.Square, accum_out=sm[0:C, SS, h:h + 1]),
                  reads=[o[0:C, h, :]], writes=[X[0][0:C, h % 4, 0:128], sm[0:C, SS, h:h + 1]])
        P.act(sv(RSTD), sv(SS), AF.Sqrt, bias=k.epsc[0:C, 0:1], scale=1.0 / 128)
        P.recip("dve", sv(RSTD), sv(RSTD))
        P.tt("dve", o[0:C, :, :], o[0:C, :, :], bc_last(sv(RSTD).rearrange("p (h o) -> p h o", o=1), 128), mul)
        for h in range(8):
            ps = k.ps[h // 4]
            P.tr(ps[:, (h % 4) * C:(h % 4 + 1) * C], o[0:C, h, :], idC)
        for hg in range(2):
            P.stt("dve", OT[:, hg * 4:hg * 4 + 4, cols], k.ps[hg][:, 0:4 * C].rearrange("p (h c) -> p h c", c=C), nw[:, 0:1],
                  Zs[:, hg * 4:hg * 4 + 4, cols], mul, mul)

    for ti, (c0, n, kind) in enumerate(GDN_TILES):
        proj_tile(c0, n, kind)
        if kind == "p":
            for cc in range(n // 128):
                chunk(cc * 128, 128, 1)
            if c0 + n == NTP:
                P.dma("sp", O["p_gdn"][j].rearrange("h k v -> k h v"), S[:, :, :])
        else:
            chunk(0, NS, NSB)
        def oload(oc_):
            P.dma("pool", Wo[oc_ % 2][:, :, :], I["gdn_w_out"][j][:, oc_ * 128:(oc_ + 1) * 128].rearrange("(kc k) m -> k kc m", k=128))

        oload(0)
        for oc in range(8):
            if oc + 1 < 8:
                oload(oc + 1)
            Wc = Wo[oc % 2]
            ps = k.ps[6 + oc % 2]
            for kc in range(8):
                P.mm(ps[:, 0:n], Wc[:, kc, :], OT[:, kc, 0:n], start=(kc == 0), stop=(kc == 7))
            P.tt("dve", k.XR[:, oc, c0:c0 + n], k.XR[:, oc, c0:c0 + n], ps[:, 0:n], add)
    _emit_tail(k, gtail, 24, O["p_gdn_conv"][j], O["s_gdn_conv"][j].rearrange("b r f -> (b r) f"), 3)


_WNAMES = ["norm_mix", "norm_ffn", "norm_final", "s5_w_in", "s5_a_re", "s5_a_im", "s5_log_dt", "s5_b_re", "s5_b_im",
           "s5_c_re", "s5_c_im", "s5_d", "s5_w_glu", "lru_w_in", "lru_conv_w", "lru_conv_b", "lru_w_gate_a",
           "lru_b_gate_a", "lru_w_gate_x", "lru_b_gate_x", "lru_lambda", "lru_w_out", "gdn_w_in", "gdn_conv_w",
           "gdn_a_log", "gdn_dt_bias", "gdn_norm", "gdn_w_out", "ffn_w_up", "ffn_conv_w", "ffn_conv_b", "ffn_w_down"]


def make_in_maps(inp):
    maps = []
    c = np.ascontiguousarray
    for core in range(8):
        sl = slice(core * NSB, (core + 1) * NSB)
        m = {
            "xp": c(inp["x_prompt"][core]),
            "xs": c(inp["x_sample"][sl].reshape(NS, D)),
            "st_s5_re": c(inp["state_s5_re"][:, sl]),
            "st_s5_im": c(inp["state_s5_im"][:, sl]),
            "st_lru": c(inp["state_lru"][:, sl]),
            "st_lru_conv": c(inp["state_lru_conv"][:, sl]),
            "st_gdn": c(inp["state_gdn"][:, sl]),
            "st_gdn_conv": c(inp["state_gdn_conv"][:, sl]),
            "st_ffn_conv": c(inp["state_ffn_conv"][:, sl]),
        }
        for n_ in _WNAMES:
            m[n_] = c(np.asarray(inp[n_], dtype=np.float32))
        maps.append(m)
    return maps


_NC_CACHE = {}


def kernel(**inputs):
    inp = {k_: np.asarray(v) for k_, v in inputs.items()}
    if "nc" not in _NC_CACHE:
        _NC_CACHE["nc"] = build()
    nc = _NC_CACHE["nc"]
    in_maps = make_in_maps(inp)
    res = run_bass_kernel_spmd(nc, in_maps, core_ids=list(range(8)))
    R = res.results

    def cat_p(name):
        return np.stack([np.asarray(r[name], dtype=np.float32) for r in R], axis=1)

    def cat_s(name):
        return np.concatenate([np.asarray(r[name], dtype=np.float32) for r in R], axis=1)

    y_prompt = np.stack([np.asarray(r["yp"], dtype=np.float32) for r in R], axis=0)
    y_sample = np.concatenate([np.asarray(r["ys"], dtype=np.float32).reshape(NSB, NST, D) for r in R], axis=0)
    return (y_prompt, y_sample,
            cat_p("p_s5_re"), cat_p("p_s5_im"), cat_p("p_lru"), cat_p("p_lru_conv"), cat_p("p_gdn"),
            cat_p("p_gdn_conv"), cat_p("p_ffn_conv"),
            cat_s("s_s5_re"), cat_s("s_s5_im"), cat_s("s_lru"), cat_s("s_lru_conv"), cat_s("s_gdn"),
            cat_s("s_gdn_conv"), cat_s("s_ffn_conv"))
```

```python
import numpy as np
import concourse.bass as bass
import concourse.mybir as mybir

F32 = mybir.dt.float32
BF16 = mybir.dt.bfloat16
I32 = mybir.dt.int32
AF = mybir.ActivationFunctionType
ALU = mybir.AluOpType
AX = mybir.AxisListType

ENGS = ("pe", "act", "dve", "pool", "sp")
_DSZ = {F32: 4, BF16: 2, I32: 4}
SEM_BLK = 2000
N_DMA_SEMS = 8
EMBED_WAIT = True


class Op:
    __slots__ = ("eng", "fn", "idx", "deps", "sig", "gcount", "is_dma", "dsem", "dval", "tag")

    def __init__(self, eng, fn, idx, is_dma, tag):
        self.eng = eng
        self.fn = fn
        self.idx = idx
        self.deps = set()
        self.sig = False
        self.gcount = 0
        self.is_dma = is_dma
        self.dsem = None
        self.dval = 0
        self.tag = tag


class Prog:
    def __init__(self, nc):
        self.nc = nc
        self.q = {e: [] for e in ENGS}
        self.recs = {}
        self.dma_count = {e: 0 for e in ENGS}
        self.dma_hist = {e: [] for e in ENGS}

    @staticmethod
    def _acc(ap):
        t = ap.tensor
        aps = ap.ap
        off = int(ap.offset)
        if isinstance(t, bass.DRamTensorHandle) or "DRam" in type(t).__name__:
            return None
        pstep, pcnt = aps[0]
        if pstep == 0:
            pstep = 1 << 40
        plo = off // pstep
        flo = off % pstep
        fhi = flo + sum((c - 1) * abs(s) for s, c in aps[1:]) + 1
        dsz = _DSZ[ap.dtype]
        if "PSum" in type(t).__name__:
            return (t.name, plo // 32 * 32, (plo + pcnt + 31) // 32 * 32, 0, 2048)
        return (t.name, plo, plo + pcnt, flo * dsz, fhi * dsz)

    def add(self, eng, fn, reads=(), writes=(), dma=False, tag=None):
        op = Op(eng, fn, len(self.q[eng]), dma, tag)
        deps = op.deps
        racc = [a for a in (self._acc(r) for r in reads) if a is not None]
        wacc = [a for a in (self._acc(w) for w in writes) if a is not None]
        for (name, plo, phi, flo, fhi) in racc:
            for r in self.recs.get(name, ()):
                if r[5] and r[0] < phi and plo < r[1] and r[2] < fhi and flo < r[3]:
                    deps.add(r[4])
        for (name, plo, phi, flo, fhi) in wacc:
            for r in self.recs.get(name, ()):
                if r[0] < phi and plo < r[1] and r[2] < fhi and flo < r[3]:
                    deps.add(r[4])
        deps.discard(op)
        for (name, plo, phi, flo, fhi) in wacc:
            lst = self.recs.setdefault(name, [])
            new = []
            for r in lst:
                if r[0] < phi and plo < r[1] and r[2] < fhi and flo < r[3] and r[0] >= plo and r[1] <= phi:
                    if r[2] < flo:
                        new.append([r[0], r[1], r[2], flo, r[4], r[5]])
                    if r[3] > fhi:
                        new.append([r[0], r[1], fhi, r[3], r[4], r[5]])
                else:
                    new.append(r)
            new.append([plo, phi, flo, fhi, op, True])
            self.recs[name] = new
        for (name, plo, phi, flo, fhi) in racc:
            lst = self.recs.setdefault(name, [])
            if not dma:
                lst = [r for r in lst if not ((not r[5]) and r[4].eng == eng and (not r[4].is_dma)
                                              and r[0] >= plo and r[1] <= phi and r[2] >= flo and r[3] <= fhi)]
            lst.append([plo, phi, flo, fhi, op, False])
            self.recs[name] = lst
        if dma:
            hist = self.dma_hist[eng]
            k = len(hist)
            if k >= N_DMA_SEMS:
                deps.add(hist[k - N_DMA_SEMS])
            hist.append(op)
        self.q[eng].append(op)
        return op

    def emit(self, block_engines):
        nc = self.nc
        for e in ENGS:
            for op in self.q[e]:
                for d in op.deps:
                    if d.is_dma:
                        continue
                    if d.eng == "pe" and op.eng == "pe" and not op.is_dma:
                        continue
                    d.sig = True
        nsig = {}
        for e in ENGS:
            n = 0
            for op in self.q[e]:
                if op.is_dma:
                    continue
                if op.sig:
                    n += 1
                    op.gcount = n
            nsig[e] = n
        return nsig


def run_prog(nc, P, final_wait=True):
    import contextlib
    nsig = P.emit(None)
    with contextlib.ExitStack() as st:
        sems = {}
        for e in ENGS:
            nb = max(1, (nsig[e] + SEM_BLK - 1) // SEM_BLK)
            sems[e] = [st.enter_context(nc.semaphore(f"s_{e}_{i}")) for i in range(nb)]
        dsems = {}
        for e in ENGS:
            if P.dma_hist[e]:
                dsems[e] = [st.enter_context(nc.semaphore(f"d_{e}_{i}")) for i in range(N_DMA_SEMS)]
                for k, op in enumerate(P.dma_hist[e]):
                    op.dsem = dsems[e][k % N_DMA_SEMS]
                    op.dval = 16 * (k // N_DMA_SEMS + 1)
        block = st.enter_context(nc.Block())

        def make(e):
            def body(eng):
                waited_c = {}
                waited_d = {}
                for op in P.q[e]:
                    need_c = {}
                    for d in op.deps:
                        if d.is_dma:
                            key = id(d.dsem)
                            if waited_d.get(key, 0) < d.dval:
                                waited_d[key] = d.dval
                                eng.wait_ge(d.dsem, d.dval)
                        else:
                            if d.eng == "pe" and e == "pe" and not op.is_dma:
                                continue
                            if d.gcount > need_c.get(d.eng, 0):
                                need_c[d.eng] = d.gcount
                    pend = []
                    for se, g in need_c.items():
                        if waited_c.get(se, 0) < g:
                            waited_c[se] = g
                            b = (g - 1) // SEM_BLK
                            pend.append((sems[se][b], (g - 1) % SEM_BLK + 1))
                    emb = None
                    if EMBED_WAIT and pend and not op.is_dma and op.tag == "w":
                        emb = pend.pop()
                    for s_, v_ in pend:
                        eng.wait_ge(s_, v_)
                    ins = op.fn(eng)
                    if emb is not None:
                        ins._wait_ge(emb[0], emb[1])
                    if op.is_dma:
                        ins.then_inc(op.dsem, 16)
                    elif op.sig:
                        b = (op.gcount - 1) // SEM_BLK
                        ins.then_inc(sems[e][b], 1)
                if e == "sp" and final_wait:
                    for qe in ENGS:
                        hist = P.dma_hist[qe]
                        for k in range(max(0, len(hist) - N_DMA_SEMS), len(hist)):
                            op = hist[k]
                            eng.wait_ge(op.dsem, op.dval)
            return body

        block.tensor(make("pe"))
        block.scalar(make("act"))
        block.vector(make("dve"))
        block.gpsimd(make("pool"))
        block.sync(make("sp"))


def _aps(*xs):
    return [x for x in xs if x is not None and not isinstance(x, (int, float))]


class PX(Prog):
    def mm(self, out, lhsT, rhs, start=True, stop=True, tag=None, **kw):
        if tag is None and getattr(self, "pe_embed", False) and lhsT.dtype == BF16 and not kw:
            tag = "w"
        return self.add("pe", lambda e: e.matmul(out, lhsT, rhs, start=start, stop=stop, **kw),
                        reads=[lhsT, rhs], writes=[out], tag=tag)

    def tr(self, out, in_, ident):
        return self.add("pe", lambda e: e.transpose(out, in_, ident), reads=[in_, ident], writes=[out])

    def act(self, out, in_, func, bias=None, scale=None, eng="act"):
        kw = {}
        if bias is not None:
            kw["bias"] = bias
        if scale is not None:
            kw["scale"] = scale
        return self.add(eng, lambda e: e.activation(out, in_, func, **kw),
                        reads=_aps(in_, bias, scale), writes=[out], tag=("w" if not _aps(bias, scale) else None))

    def tt(self, eng, out, in0, in1, op):
        return self.add(eng, lambda e: e.tensor_tensor(out, in0, in1, op), reads=[in0, in1], writes=[out], tag="w")

    def ts(self, eng, out, in0, s1, s2, op0, op1=None):
        if op1 is None:
            return self.add(eng, lambda e: e.tensor_scalar(out, in0, s1, None, op0),
                            reads=_aps(in0, s1), writes=[out], tag=("w" if not _aps(s1) else None))
        return self.add(eng, lambda e: e.tensor_scalar(out, in0, s1, s2, op0, op1),
                        reads=_aps(in0, s1, s2), writes=[out], tag=("w" if not _aps(s1, s2) else None))

    def stt(self, eng, out, in0, scalar, in1, op0, op1):
        return self.add(eng, lambda e: e.scalar_tensor_tensor(out, in0, scalar, in1, op0, op1),
                        reads=_aps(in0, scalar, in1), writes=[out], tag=("w" if not _aps(scalar) else None))

    def copy(self, eng, out, in_):
        if eng == "act":
            return self.add(eng, lambda e: e.copy(out, in_), reads=[in_], writes=[out], tag="w")
        return self.add(eng, lambda e: e.tensor_copy(out, in_), reads=[in_], writes=[out], tag="w")

    def scan(self, eng, out, d0, d1, init, op0, op1):
        return self.add(eng, lambda e: e.tensor_tensor_scan(out, d0, d1, init, op0, op1),
                        reads=_aps(d0, d1, init), writes=[out], tag=("w" if not _aps(init) else None))

    def memset(self, eng, ap, val):
        return self.add(eng, lambda e: e.memset(ap, val), writes=[ap])

    def dma(self, eng, out, in_, **kw):
        return self.add(eng, lambda e: e.dma_start(out=out, in_=in_, **kw), reads=[in_], writes=[out], dma=True)

    def recip(self, eng, out, in_):
        return self.add(eng, lambda e: e.reciprocal(out, in_), reads=[in_], writes=[out])


import contextlib
from concourse.bass_utils import run_bass_kernel_spmd

NTP = 2048
NSB = 16
NST = 4
NS = NSB * NST
NT = NTP + NS
D = 1024
DC = 8
FH = 2816
F2 = 5632
NPAIR = 22
DEPTH = 4
LRU_W = 1280
LRU_C = 10
GQ = 3072
EPS = 1e-6

TILES = [(i * 512, 512, "p") for i in range(4)] + [(NTP, NS, "s")]


class K:
    pass


def build(layers_cfg=None, n_depth=DEPTH, dbg=''):
    nc = bass.Bass("TRN2", target_bir_lowering=False)
    k = K()
    k.nc = nc
    P = PX(nc)
    k.P = P
    st = contextlib.ExitStack()
    k.st = st

    def din(name, shape):
        return nc.dram_tensor(name, list(shape), F32, kind="ExternalInput").ap()

    def dout(name, shape):
        return nc.dram_tensor(name, list(shape), F32, kind="ExternalOutput").ap()

    def sb(name, shape, dt=F32):
        return st.enter_context(nc.sbuf_tensor(name, list(shape), dt))

    k.sb = sb
    ARENA_BYTES = 93 * 1024
    k.AR = sb("AR", [128, ARENA_BYTES // 4], F32)
    k.ar_off = 0

    def alloc(shape, dt=F32, at=None):
        dsz = _DSZ[dt]
        n = 1
        for d_ in shape[1:]:
            n *= d_
        nbytes = (n * dsz + 31) // 32 * 32
        off = k.ar_off if at is None else at
        assert off + nbytes <= ARENA_BYTES, (off, nbytes, shape)
        if at is None:
            k.ar_off = off + nbytes
        v = k.AR[:, off // 4:(off + nbytes) // 4]
        if dt != F32:
            v = v.bitcast(dt)
        v = v[:, 0:n]
        if len(shape) > 2:
            names = " ".join(f"d{i}" for i in range(len(shape) - 1))
            kw = {f"d{i}": shape[i + 1] for i in range(len(shape) - 1)}
            v = v.rearrange(f"p ({names}) -> p {names}", **kw)
        if shape[0] < 128:
            v = v[0:shape[0]]
        return v

    k.alloc = alloc
    I = {}
    I["xp"] = din("xp", [NTP, D])
    I["xs"] = din("xs", [NS, D])
    I["st_s5_re"] = din("st_s5_re", [2, NSB, 64, 64])
    I["st_s5_im"] = din("st_s5_im", [2, NSB, 64, 64])
    I["st_lru"] = din("st_lru", [1, NSB, LRU_W])
    I["st_lru_conv"] = din("st_lru_conv", [1, NSB, 3, LRU_W])
    I["st_gdn"] = din("st_gdn", [1, NSB, 8, 128, 128])
    I["st_gdn_conv"] = din("st_gdn_conv", [1, NSB, 3, GQ])
    I["st_ffn_conv"] = din("st_ffn_conv", [DEPTH, NSB, 2, F2])
    wshapes = dict(
        norm_mix=[DEPTH, D], norm_ffn=[DEPTH, D], norm_final=[D],
        s5_w_in=[2, D, D], s5_a_re=[2, 64, 64], s5_a_im=[2, 64, 64], s5_log_dt=[2, 64],
        s5_b_re=[2, 64, 64, 16], s5_b_im=[2, 64, 64, 16], s5_c_re=[2, 64, 16, 64], s5_c_im=[2, 64, 16, 64],
        s5_d=[2, D], s5_w_glu=[2, D, 2 * D],
        lru_w_in=[1, D, 2 * LRU_W], lru_conv_w=[1, 4, LRU_W], lru_conv_b=[1, LRU_W],
        lru_w_gate_a=[1, 10, 128, 128], lru_b_gate_a=[1, LRU_W], lru_w_gate_x=[1, 10, 128, 128],
        lru_b_gate_x=[1, LRU_W], lru_lambda=[1, LRU_W], lru_w_out=[1, LRU_W, D],
        gdn_w_in=[1, D, 4112], gdn_conv_w=[1, 4, GQ], gdn_a_log=[1, 8], gdn_dt_bias=[1, 8],
        gdn_norm=[1, 128], gdn_w_out=[1, D, D],
        ffn_w_up=[DEPTH, D, F2], ffn_conv_w=[DEPTH, 3, F2], ffn_conv_b=[DEPTH, F2], ffn_w_down=[DEPTH, FH, D],
    )
    for n_, s_ in wshapes.items():
        I[n_] = din(n_, s_)
    O = {}
    O["yp"] = dout("yp", [NTP, D])
    O["ys"] = dout("ys", [NS, D])
    O["p_s5_re"] = dout("p_s5_re", [2, 64, 64])
    O["p_s5_im"] = dout("p_s5_im", [2, 64, 64])
    O["p_lru"] = dout("p_lru", [1, LRU_W])
    O["p_lru_conv"] = dout("p_lru_conv", [1, 3, LRU_W])
    O["p_gdn"] = dout("p_gdn", [1, 8, 128, 128])
    O["p_gdn_conv"] = dout("p_gdn_conv", [1, 3, GQ])
    O["p_ffn_conv"] = dout("p_ffn_conv", [DEPTH, 2, F2])
    O["s_s5_re"] = dout("s_s5_re", [2, NSB, 64, 64])
    O["s_s5_im"] = dout("s_s5_im", [2, NSB, 64, 64])
    O["s_lru"] = dout("s_lru", [1, NSB, LRU_W])
    O["s_lru_conv"] = dout("s_lru_conv", [1, NSB, 3, LRU_W])
    O["s_gdn"] = dout("s_gdn", [1, NSB, 8, 128, 128])
    O["s_gdn_conv"] = dout("s_gdn_conv", [1, NSB, 3, GQ])
    O["s_ffn_conv"] = dout("s_ffn_conv", [DEPTH, NSB, 2, F2])
    k.I, k.O = I, O

    k.XR = sb("XR", [128, DC, NT], F32)
    k.XN = sb("XN", [128, DC, NT], BF16)
    k.ident = sb("ident", [128, 128], F32)
    k.identb = sb("identb", [128, 128], BF16)
    k.onesb = sb("onesb", [128, 128], BF16)
    k.gam = sb("gam", [128, 2 * DEPTH + 1, DC], F32)
    k.ps = [st.enter_context(nc.psum_tensor(f"ps{i}", [128, 512], F32)) for i in range(8)]
    k.scr = sb("scr", [128, 2816], F32)
    k.stage = k.scr[:, 0:1024]
    k.stage2 = k.scr[:, 1024:2048]

    k.rstd = sb("rstd", [128, 512], F32)
    k.epsc = sb("epsc", [128, 1], F32)
    P.pe_embed = True
    _consts(k)
    if 's_consts' in dbg:
        run_prog(nc, P); st.close(); return nc
    _load_x(k)
    if 's_load' in dbg:
        run_prog(nc, P); st.close(); return nc
    if layers_cfg is None:
        layers_cfg = [("s5", 0), ("lru", 0), ("gdn", 0), ("s5", 1)][:n_depth]
    k.dbg = dbg
    if 'nolayers' in dbg:
        layers_cfg = []
    for li, (kind, j) in enumerate(layers_cfg):
        if kind is not None:
            _rmsnorm(k, gi=li)
            if kind == "s5":
                _s5(k, li, j)
            elif kind == "lru":
                _lru(k, li, j)
            elif kind == "gdn":
                P.pe_embed = False
                _gdn(k, li, j)
                P.pe_embed = True
        _rmsnorm(k, gi=DEPTH + li)
        if 'noffn' not in dbg:
            _ffn(k, li)
    _final(k)
    run_prog(nc, P)
    st.close()
    return nc


def _consts(k):
    P, nc = k.P, k.nc
    it = k.sb("iota_i", [128, 128], I32)
    P.add("pool", lambda e: e.iota(it[:], [[1, 128]], base=0, channel_multiplier=-1), writes=[it[:]])
    k.jmp = k.sb("jmp", [128, 128], F32)
    P.copy("dve", k.jmp[:], it[:])
    P.ts("dve", k.ident[:], k.jmp[:], 0.0, None, ALU.is_equal)
    P.copy("dve", k.identb[:], k.ident[:])
    P.memset("dve", k.onesb[:], 1.0)
    P.memset("dve", k.epsc[:], EPS)
    k.halfpi = k.sb("halfpi", [128, 1], F32)
    P.memset("dve", k.halfpi[:], float(np.pi / 2))
    k.onec = k.sb("onec", [128, 1], F32)
    P.memset("dve", k.onec[:], 1.0)
    rows = []
    for i in range(DEPTH):
        rows.append(k.I["norm_mix"][i:i + 1, :])
    for i in range(DEPTH):
        rows.append(k.I["norm_ffn"][i:i + 1, :])
    rows.append(k.I["norm_final"].rearrange("(o d) -> o d", o=1))
    load_cols(k, rows, D, k.gam)


def load_cols(k, rows, n, dst, eng="sp"):
    P = k.P
    R = len(rows)
    nchunk = n // 128
    stg = k.scr
    assert R <= 16 and n <= 2816
    for r, ap in enumerate(rows):
        P.dma(eng, stg[r:r + 1, 0:n], ap)
    done = 0
    while done < nchunk:
        m = min(nchunk - done, 512 // R)
        ps = k.ps[7]
        for c in range(m):
            P.tr(ps[:, c * R:(c + 1) * R], stg[0:R, (done + c) * 128:(done + c + 1) * 128], k.ident[0:R, 0:R])
        P.copy("dve", dst[:, :, done:done + m].rearrange("p r c -> p c r"),
               ps[:, 0:m * R].rearrange("p (c r) -> p c r", r=R))
        done += m


def _load_x(k):
    P = k.P
    srcs = [(k.I["xp"], i * 128, 128, i * 128) for i in range(NTP // 128)] + [(k.I["xs"], 0, NS, NTP)]
    for bi, (src, r0, n, c0) in enumerate(srcs):
        stg = k.stage if bi % 2 == 0 else k.stage2
        P.dma("sp", stg[0:n, :], src[r0:r0 + n, :])
        for half in range(2):
            ps = k.ps[(bi * 2 + half) % 4]
            for c in range(4):
                ch = half * 4 + c
                P.tr(ps[:, c * n:(c + 1) * n], stg[0:n, ch * 128:(ch + 1) * 128], k.ident[0:n, 0:n])
            eng = "dve" if half == 0 else "act"
            P.copy(eng, k.XR[:, half * 4:half * 4 + 4, c0:c0 + n],
                   ps[:, 0:4 * n].rearrange("p (c t) -> p c t", t=n))


def _rmsnorm(k, gi, out_f32=None):
    P = k.P
    k.ar_off = 0
    sqb = k.alloc([128, DC, 512], BF16)
    for ti, (c0, n, kind) in enumerate(TILES):
        sqv = sqb[:, :, 0:n]
        for c in range(DC):
            P.act(sqv[:, c, :], k.XR[:, c, c0:c0 + n], AF.Square)
        ps = k.ps[6 + (ti % 2)]
        for c in range(DC):
            P.mm(ps[:, 0:n], k.onesb[:], sqv[:, c, :], start=(c == 0), stop=(c == DC - 1))
        rs = k.rstd[:, 0:n]
        P.act(rs, ps[:, 0:n], AF.Sqrt, bias=k.epsc[:, 0:1], scale=1.0 / D)
        P.recip("dve", rs, rs)
        for c in range(DC):
            if out_f32 is None:
                P.stt("dve", k.XN[:, c, c0:c0 + n], k.XR[:, c, c0:c0 + n],
                      k.gam[:, gi, c:c + 1], rs, ALU.mult, ALU.mult)
            else:
                P.stt("dve", out_f32[:, c, 0:n], k.XR[:, c, c0:c0 + n],
                      k.gam[:, gi, c:c + 1], rs, ALU.mult, ALU.mult)
        if out_f32 is not None and 'nostore' not in k.dbg:
            if ('onlyp' in k.dbg and kind != 'p') or ('onlys' in k.dbg and kind != 's'):
                continue
            _store_y(k, out_f32, c0, n, kind)


def _store_y(k, yf, c0, n, kind):
    P = k.P
    dst = k.O["yp"] if kind == "p" else k.O["ys"]
    r0 = c0 if kind == "p" else 0
    nb = (n + 127) // 128
    for b in range(nb):
        m = min(128, n - b * 128)
        stg = k.stage if b % 2 == 0 else k.stage2
        for half in range(2):
            ps = k.ps[4 + half]
            for c in range(4):
                ch = half * 4 + c
                P.tr(ps[0:m, c * 128:(c + 1) * 128], yf[:, ch, b * 128:b * 128 + m], k.ident[:, :])
            P.copy("act" if half == 0 else "dve", stg[0:m, half * 512:(half + 1) * 512], ps[0:m, :])
        P.dma("sp", dst[r0 + b * 128:r0 + b * 128 + m, :], stg[0:m, :])


def _final(k):
    yf = k.alloc([128, DC, 512], F32, at=16 * 1024)
    _rmsnorm(k, gi=2 * DEPTH, out_f32=yf)


FFN_GROUPS = [list(range(i, min(i + 3, NPAIR))) for i in range(0, NPAIR, 3)]


def _ffn_setup(k):
    al = k.alloc
    k.ar_off = 8 * 1024
    k.WA = [al([128, 3, 3072], BF16) for i in range(2)]
    k.fcw = al([128, 4, 44], F32)
    k.fdiag = al([128, 3, 2, 3, 128], BF16)
    k.hext = al([128, 3, 2, 32 + 512], BF16)
    k.hexs = al([128, 3, 2, NSB * 6], BF16)
    k.ga = al([128, 2, 512], F32)
    k.G = al([128, 3, 512], BF16)
    k.ftail = al([128, 44, 2 + 2 * NSB], F32)
    k.fst = al([32, 768], F32)


def _ffn(k, li):
    P, I = k.P, k.I
    _ffn_setup(k)
    rows = [I["ffn_conv_w"][li, r:r + 1, :] for r in range(3)] + [I["ffn_conv_b"][li:li + 1, :]]
    for h in range(2):
        rr = [r_[:, h * FH:(h + 1) * FH] for r_ in rows]
        load_cols(k, rr, FH, k.fcw[:, :, h * 22:(h + 1) * 22])
    if 'ffn_a' in k.dbg:
        return
    for gi, grp in enumerate(FFN_GROUPS):
        if 'ffn_g1' in k.dbg and gi >= 1:
            break
        WA = k.WA[gi % 2]
        for s, p in enumerate(grp):
            for ab in range(2):
                col0 = ab * FH + p * 128
                src = I["ffn_w_up"][li][:, col0:col0 + 128].rearrange("(kc k) m -> k kc m", k=128)
                P.dma("pool", WA[:, s, ab * 1024:(ab + 1) * 1024].rearrange("k (kc m) -> k kc m", m=128), src)
            P.dma("pool", WA[:, s, 2048:3072], I["ffn_w_down"][li][p * 128:(p + 1) * 128, :])
        for s, p in enumerate(grp):
            for ab in range(2):
                col0 = ab * FH + p * 128
                P.dma("sp", k.fst[:, (s * 2 + ab) * 128:(s * 2 + ab + 1) * 128],
                      I["st_ffn_conv"][li].rearrange("b r f -> (b r) f")[:, col0:col0 + 128])
        for s, p in enumerate(grp):
            for ab in range(2):
                ch = ab * 22 + p
                for t in range(3):
                    P.ts("dve", k.fdiag[:, s, ab, t, :], k.ident[:], k.fcw[:, t, ch:ch + 1], None, ALU.mult)
                P.memset("dve", k.hext[:, s, ab, 30:32], 0.0)
                ps = k.ps[7]
                P.tr(ps[:, 0:32], k.fst[0:32, (s * 2 + ab) * 128:(s * 2 + ab + 1) * 128], k.ident[0:32, 0:32])
                P.copy("dve", k.hexs[:, s, ab, :].rearrange("p (b j) -> p b j", j=6)[:, :, 0:2],
                       ps[:, 0:32].rearrange("p (b r) -> p b r", r=2))
        if 'ffn_b' in k.dbg:
            continue
        def tile_fns(c0, n, kind):
            last_p = (kind == "p" and c0 + n == NTP)

            def stage_u(s, p, set_):
                for ab in range(2):
                    ps = k.ps[set_ * 2 + ab]
                    for kc in range(DC):
                        P.mm(ps[:, 0:n], WA[:, s, ab * 1024 + kc * 128: ab * 1024 + (kc + 1) * 128],
                             k.XN[:, kc, c0:c0 + n], start=(kc == 0), stop=(kc == DC - 1))
                    ch = ab * 22 + p
                    if 'noevac' in k.dbg:
                        continue
                    if kind == "p":
                        P.copy("dve" if 'evdve' in k.dbg else "act", k.hext[:, s, ab, 32:32 + n], ps[:, 0:n])
                        if last_p:
                            P.copy("dve" if 'evdve' in k.dbg else "act", k.ftail[:, ch, 0:2], ps[:, n - 2:n])
                    else:
                        P.copy("dve", k.hexs[:, s, ab, :].rearrange("p (b j) -> p b j", j=6)[:, :, 2:6],
                               ps[:, 0:n].rearrange("p (b t) -> p b t", t=NST))
                        P.copy("dve", k.ftail[:, ch, 2:2 + 2 * NSB].rearrange("p (b r) -> p b r", r=2),
                               ps[:, 0:n].rearrange("p (b t) -> p b t", t=NST)[:, :, 2:4])

            def stage_c(s, p):
                for ab in range(2):
                    ps = k.ps[4 + ab]
                    for t in range(3):
                        if kind == "p":
                            rhs = k.hext[:, s, ab, 30 + t:30 + t + n]
                        else:
                            rhs = k.hexs[:, s, ab, :].rearrange("p (b j) -> p b j", j=6)[:, :, t:t + NST]
                        P.mm(ps[:, 0:n], k.fdiag[:, s, ab, t, :], rhs, start=(t == 0), stop=(t == 2))
                ga = k.ga[:, s % 2, 0:n]
                P.act(ga, k.ps[4][:, 0:n], AF.Gelu_apprx_tanh, bias=k.fcw[:, 3, p:p + 1])
                P.stt("dve", k.G[:, s, 0:n], k.ps[5][:, 0:n], k.fcw[:, 3, 22 + p:22 + p + 1], ga, ALU.add, ALU.mult)
                if kind == "p" and not last_p and 'nohalo' not in k.dbg:
                    for ab in range(2):
                        P.copy("pool", k.hext[:, s, ab, 30:32], k.hext[:, s, ab, 30 + n:32 + n])

            def stage_d():
                ns_ = len(grp)
                for oc in range(DC):
                    ps = k.ps[6 + oc % 2]
                    for s in range(ns_):
                        P.mm(ps[:, 0:n], WA[:, s, 2048 + oc * 128:2048 + (oc + 1) * 128], k.G[:, s, 0:n],
                             start=(s == 0), stop=(s == ns_ - 1))
                    P.tt("dve", k.XR[:, oc, c0:c0 + n], k.XR[:, oc, c0:c0 + n], ps[:, 0:n], ALU.add)

            return stage_u, stage_c, stage_d

        ns = len(grp)
        fns = [tile_fns(*t) for t in TILES]
        items = [(ti, s) for ti in range(len(TILES)) for s in range(ns)]
        fns[0][0](0, grp[0], 0)
        for i, (ti, s) in enumerate(items):
            nxt = items[i + 1] if i + 1 < len(items) else None
            if nxt is not None and nxt[1] != s:
                fns[nxt[0]][0](nxt[1], grp[nxt[1]], (i + 1) % 2)
            fns[ti][1](s, grp[s])
            if nxt is not None and nxt[1] == s:
                fns[nxt[0]][0](nxt[1], grp[nxt[1]], (i + 1) % 2)
            if s == ns - 1:
                fns[ti][2]()
    if 'notail' in k.dbg:
        return
    _emit_tail(k, k.ftail, 44, k.O["p_ffn_conv"][li], k.O["s_ffn_conv"][li].rearrange("b r f -> (b r) f"), 2)


def _emit_tail(k, tail, nchunk, dst_p, dst_s, R):
    P = k.P
    W = R + R * NSB
    done = 0
    while done < nchunk:
        m = min(4, nchunk - done)
        ps = k.ps[7]
        for c in range(m):
            P.tr(ps[0:W, c * 128:(c + 1) * 128], tail[:, done + c, 0:W], k.ident[:, :])
        stg = k.stage
        P.copy("dve", stg[0:W, 0:m * 128], ps[0:W, 0:m * 128])
        P.dma("sp", dst_p[:, done * 128:(done + m) * 128], stg[0:R, 0:m * 128])
        P.dma("sp", dst_s[:, done * 128:(done + m) * 128], stg[R:W, 0:m * 128])
        done += m


def bc_last(ap, m):
    a = [list(x) for x in ap.ap]
    assert a[-1][1] == 1
    a[-1] = [0, m]
    return bass.AP(ap.tensor, ap.offset, a)


def ins_bc(ap, pos, m):
    a = [list(x) for x in ap.ap]
    a.insert(pos, [0, m])
    return bass.AP(ap.tensor, ap.offset, a)


S5_TC = 32
S5_TILES = [(i * 256, 256, "p") for i in range(8)] + [(NTP, NS, "s")]


def _s5(k, li, j):
    P, I, O, al = k.P, k.I, k.O, k.alloc
    TC = S5_TC
    k.ar_off = 8 * 1024
    mul, add, sub = ALU.mult, ALU.add, ALU.subtract
    LQ = al([128, 32, 2, 128], BF16)
    CT = al([128, 2, 8, 128], BF16)
    cosB = al([128, 32, TC], BF16)
    sinB = al([128, 32, TC], BF16)
    cl = al([128, 32], F32)
    sl = al([128, 32], F32)
    c4 = al([128, 32, 4], F32)
    s4 = al([128, 32, 4], F32)
    amul = al([128, 32, TC], F32)
    dd = al([128, 8, 128], BF16)
    h0 = [al([128, NSB, 32], F32) for _ in range(2)]
    hfin = h0
    hprev = [al([128, 32], F32) for _ in range(2)]
    er = al([128, 32], F32)
    sm = al([128, 12, 32], F32)
    dcl = al([128, 1, 8], F32)
    uoff = k.ar_off
    U = al([128, 8, 256], BF16)
    G = al([128, 8, 256], BF16)
    xoff = k.ar_off
    G1 = al([128, 32, TC], F32)
    G2 = al([128, 32, TC], F32)
    S1 = al([128, 32, TC], F32)
    S2 = al([128, 32, TC], F32)
    XBd = [al([128, 2, 32, TC], BF16) for _ in range(2)]
    TB1 = al([128, 32, TC], BF16)
    TB2 = al([128, 32, TC], BF16)
    TB3 = al([128, 32, TC], BF16)
    TB4 = al([128, 32, TC], BF16)
    HS = al([128, 2, 32, TC], BF16)
    xend = k.ar_off
    k.ar_off = xoff
    Wi = [al([128, 8, 128], BF16) for _ in range(8)]
    Wg = [al([128, 2, 8, 128], BF16) for _ in range(4)]
    SG = al([128, 256], F32)
    TM = al([128, 256], F32)
    k.ar_off = uoff
    Bn = [al([128, 32, 16], F32) for _ in range(2)]
    bb = [al([128, 32, 16], F32) for _ in range(2)]
    Bpad = al([128, 32, 32], F32)
    Z = al([128, 4, 128], F32)
    Cn = [al([128, 8, 64], F32) for _ in range(2)]
    Cpad = al([128, 8, 128], F32)
    rowt = al([128, 128], F32)
    mski = al([128, 2], I32)
    mskf = al([128, 4], F32)
    assert k.ar_off <= 93 * 1024, k.ar_off
    k.ar_off = max(k.ar_off, xend)

    def S(i):
        return sm[:, i, :]

    def load_T(rows_ap, R, dst):
        P.dma("sp", rowt[0:R, :], rows_ap)
        P.tr(k.ps[7][:, 0:R], rowt[0:R, :], k.ident[0:R, 0:R])
        P.copy("dve", dst, k.ps[7][:, 0:R])

    lre, lim, ldt = S(0), S(1), S(2)
    load_T(I["s5_a_re"][j].rearrange("(q g2) p -> q (g2 p)", g2=2), 32, lre)
    load_T(I["s5_a_im"][j].rearrange("(q g2) p -> q (g2 p)", g2=2), 32, lim)
    P.dma("sp", Z[0:32, 0, 0:2], I["s5_log_dt"][j].rearrange("(q g) -> q g", g=2))
    P.copy("dve", rowt[0:32, :].rearrange("q (g p) -> q g p", g=2),
           bc_last(Z[0:32, 0, 0:2].rearrange("q (g o) -> q g o", o=1), 64))
    P.tr(k.ps[7][:, 0:32], rowt[0:32, :], k.ident[0:32, 0:32])
    P.copy("dve", ldt, k.ps[7][:, 0:32])
    dt_ = S(2)
    P.act(dt_, ldt, AF.Exp)
    th = S(3)
    P.tt("dve", th, lim, dt_, mul)
    P.tt("dve", er, lre, dt_, mul)
    P.act(er, er, AF.Exp)
    cc, ss, t_a, t_b = S(4), S(5), S(6), S(7)
    P.act(ss, th, AF.Sin, scale=1.0 / 64)
    P.act(cc, th, AF.Sin, scale=1.0 / 64, bias=k.halfpi[:, 0:1])
    for _ in range(6):
        P.tt("dve", t_a, cc, cc, mul)
        P.tt("dve", t_b, ss, ss, mul)
        P.tt("dve", ss, cc, ss, mul)
        P.ts("dve", ss, ss, 2.0, None, mul)
        P.tt("dve", cc, t_a, t_b, sub)
    are, aim = S(6), S(7)
    P.tt("dve", are, er, cc, mul)
    P.tt("dve", aim, er, ss, mul)
    am1 = S(8)
    P.ts("dve", am1, are, -1.0, None, add)
    nr, ni, den = S(9), S(10), S(11)
    P.tt("dve", nr, am1, lre, mul)
    P.tt("dve", t_a_ := S(3), aim, lim, mul)
    P.tt("dve", nr, nr, S(3), add)
    P.tt("dve", ni, aim, lre, mul)
    P.tt("dve", S(3), am1, lim, mul)
    P.tt("dve", ni, ni, S(3), sub)
    P.tt("dve", den, lre, lre, mul)
    P.tt("dve", S(3), lim, lim, mul)
    P.tt("dve", den, den, S(3), add)
    P.recip("dve", den, den)
    kre, kim = S(9), S(10)
    P.tt("dve", kre, nr, den, mul)
    P.tt("dve", kim, ni, den, mul)
    cosT, sinT = G1, G2
    P.copy("dve", cosT[:, :, 0], cc)
    P.copy("dve", sinT[:, :, 0], ss)
    m = 1
    while m < TC:
        rc = bc_last(cosT[:, :, m - 1:m], m)
        rs = bc_last(sinT[:, :, m - 1:m], m)
        tA = S1[:, :, 0:m]
        tB = S2[:, :, 0:m]
        P.tt("dve", tA, cosT[:, :, 0:m], rc, mul)
        P.tt("dve", tB, sinT[:, :, 0:m], rs, mul)
        P.tt("dve", cosT[:, :, m:2 * m], tA, tB, sub)
        P.tt("dve", tA, sinT[:, :, 0:m], rc, mul)
        P.tt("dve", tB, cosT[:, :, 0:m], rs, mul)
        P.tt("dve", sinT[:, :, m:2 * m], tA, tB, add)
        m *= 2
    P.copy("dve", cosB[:, :, :], cosT[:, :, :])
    P.copy("dve", sinB[:, :, :], sinT[:, :, :])
    P.copy("dve", cl[:, :], cosT[:, :, TC - 1])
    P.copy("dve", sl[:, :], sinT[:, :, TC - 1])
    P.copy("dve", c4[:, :, :], cosT[:, :, 0:4])
    P.copy("dve", s4[:, :, :], sinT[:, :, 0:4])
    P.copy("dve", amul[:, :, :], bc_last(er.rearrange("p (q o) -> p q o", o=1), TC))
    P.memset("dve", amul[:, :, 0:1], 0.0)
    for ri, nm in enumerate(["s5_b_re", "s5_b_im"]):
        for g2 in range(2):
            src = I[nm][j].rearrange("(q g2) p c -> g2 p q c", g2=2)[g2]
            P.dma("sp", Bn[ri][64 * g2:64 * g2 + 64, :, :], src)
    kreb = bc_last(kre.rearrange("p (q o) -> p q o", o=1), 16)
    kimb = bc_last(kim.rearrange("p (q o) -> p q o", o=1), 16)
    tb = Bpad[:, :, 0:16]
    P.tt("dve", bb[0], Bn[0], kreb, mul)
    P.tt("dve", tb, Bn[1], kimb, mul)
    P.tt("dve", bb[0], bb[0], tb, sub)
    P.tt("dve", bb[1], Bn[1], kreb, mul)
    P.tt("dve", tb, Bn[0], kimb, mul)
    P.tt("dve", bb[1], bb[1], tb, add)
    P.memset("dve", Z[:], 0.0)
    for ri in range(2):
        P.memset("dve", Bpad[:], 0.0)
        P.copy("dve", Bpad[0:64, :, 0:16], bb[ri][0:64, :, :])
        P.copy("dve", Bpad[64:128, :, 16:32], bb[ri][64:128, :, :])
        for q in range(32):
            q4 = q % 4
            P.copy("dve", Z[:, q4, 32 * q4:32 * q4 + 32], Bpad[:, q, :])
            ps = k.ps[4 + q % 2]
            P.tr(ps[:, 0:128], Z[:, q4, :], k.ident[:])
            P.copy("act", LQ[:, q, ri, :], ps[:, 0:128])
    P.add("pool", lambda e: e.iota(mski[:, 0:1], [[0, 1]], base=0, channel_multiplier=1), writes=[mski[:, 0:1]])
    P.add("dve", lambda e: e.tensor_single_scalar(mski[:, 1:2], mski[:, 0:1], 4, ALU.arith_shift_right),
          reads=[mski[:, 0:1]], writes=[mski[:, 1:2]])
    P.add("dve", lambda e: e.tensor_single_scalar(mski[:, 0:1], mski[:, 1:2], 1, ALU.bitwise_and),
          reads=[mski[:, 1:2]], writes=[mski[:, 0:1]])
    P.copy("dve", mskf[:, 1:2], mski[:, 0:1])
    P.ts("dve", mskf[:, 0:1], mskf[:, 1:2], -1.0, 1.0, mul, add)
    P.ts("dve", mskf[:, 2:3], mskf[:, 0:1], -1.0, None, mul)
    P.ts("dve", mskf[:, 3:4], mskf[:, 1:2], -1.0, None, mul)
    for ri, nm in enumerate(["s5_c_re", "s5_c_im"]):
        P.dma("sp", Cn[ri][:, :, :], I[nm][j].rearrange("(ch g8) c p -> (g8 c) ch p", g8=8))
        P.ts("dve", Cpad[:, :, 0:64], Cn[ri][:, :, :], mskf[:, 2 * ri:2 * ri + 1], None, mul)
        P.ts("dve", Cpad[:, :, 64:128], Cn[ri][:, :, :], mskf[:, 2 * ri + 1:2 * ri + 2], None, mul)
        for ch in range(8):
            ps = k.ps[4 + ch % 2]
            P.tr(ps[:, 0:128], Cpad[:, ch, :], k.ident[:])
            P.copy("act", CT[:, ri, ch, :], ps[:, 0:128])
    load_cols(k, [I["s5_d"][j:j + 1, :]], D, dcl)
    for ch in range(8):
        P.ts("dve", dd[:, ch, :], k.ident[:], dcl[:, 0, ch:ch + 1], None, mul)
    for ri, nm in enumerate(["st_s5_re", "st_s5_im"]):
        rows = I[nm][j].rearrange("b (q g2) p -> (b q) (g2 p)", g2=2)
        for blk in range(4):
            P.dma("sp", rowt[:, :], rows[blk * 128:(blk + 1) * 128, :])
            ps = k.ps[6 + blk % 2]
            P.tr(ps[:, 0:128], rowt[:, :], k.ident[:])
            P.copy("dve", h0[ri][:, blk * 4:(blk + 1) * 4, :], ps[:, 0:128].rearrange("p (b q) -> p b q", q=32))
    for ri in range(2):
        P.memset("dve", hprev[ri][:], 0.0)

    sample_tables_done = False
    for ti, (c0, n, kind) in enumerate(S5_TILES):
        for mc in range(8):
            P.dma("pool", Wi[mc][:, :, :], I["s5_w_in"][j][:, mc * 128:(mc + 1) * 128].rearrange("(kc k) m -> k kc m", k=128))
        for mc in range(8):
            Wc = Wi[mc]
            ps = k.ps[4 + mc % 2]
            for kc in range(DC):
                P.mm(ps[:, 0:n], Wc[:, kc, :], k.XN[:, kc, c0:c0 + n], start=(kc == 0), stop=(kc == DC - 1))
            P.copy("act", U[:, mc, 0:n], ps[:, 0:n])
        if kind == "s" and not sample_tables_done:
            sample_tables_done = True
            for tab, t4 in ((cosB, c4), (sinB, s4)):
                P.copy("dve", tab[:, :, :].rearrange("p q (b t) -> p q b t", t=NST), ins_bc(t4[:, :, :], 2, TC // NST))
            P.copy("dve", amul[:, :, :], bc_last(er.rearrange("p (q o) -> p q o", o=1), TC))
            P.memset("dve", amul[:, :, :].rearrange("p q (b t) -> p q b t", t=NST)[:, :, :, 0:1], 0.0)
        nsub = n // TC
        for sc in range(nsub):
            u0 = sc * TC
            for ri in range(2):
                for q in range(32):
                    ps = k.ps[ri * 2 + q // 16]
                    P.mm(ps[:, (q % 16) * TC:(q % 16 + 1) * TC], LQ[:, q, ri, :], U[:, q // 4, u0:u0 + TC])
            XB = XBd[sc % 2]
            for ri in range(2):
                for half in range(2):
                    P.copy("act", XB[:, ri, half * 16:half * 16 + 16, :], k.ps[ri * 2 + half][:, :].rearrange("p (q t) -> p q t", t=TC))
            xr, xi = XB[:, 0, :, :], XB[:, 1, :, :]
            P.tt("dve", TB1[:, :, :], xr, cosB[:, :, :], mul)
            P.tt("dve", TB2[:, :, :], xi, sinB[:, :, :], mul)
            P.tt("dve", TB3[:, :, :], xi, cosB[:, :, :], mul)
            P.tt("dve", TB4[:, :, :], xr, sinB[:, :, :], mul)
            P.tt("dve", G1[:, :, :], TB1[:, :, :], TB2[:, :, :], add)
            P.tt("dve", G2[:, :, :], TB3[:, :, :], TB4[:, :, :], sub)
            if kind == "p":
                for ri, Gx in enumerate((G1, G2)):
                    P.tt("dve", S(ri), er, hprev[ri][:], mul)
                for ri, Gx in enumerate((G1, G2)):
                    P.tt("dve", Gx[:, :, 0], Gx[:, :, 0], S(ri), add)
            else:
                nb = TC // NST
                for ri, Gx in enumerate((G1, G2)):
                    tmp = S1[:, :, 0:nb]
                    hv = h0[ri][:, sc * nb:(sc + 1) * nb, :].rearrange("p b q -> p q b")
                    P.tt("dve", tmp, hv, bc_last(er.rearrange("p (q o) -> p q o", o=1), nb), mul)
                    gv = Gx[:, :, :].rearrange("p q (b t) -> p q b t", t=NST)[:, :, :, 0]
                    P.tt("dve", gv, gv, tmp, add)
            fl = lambda t_: t_[:, :, :].rearrange("p q t -> p (q t)")
            P.scan("dve", fl(S1), fl(amul), fl(G1), 0.0, mul, add)
            P.scan("dve", fl(S2), fl(amul), fl(G2), 0.0, mul, add)
            P.tt("dve", TB1[:, :, :], S1[:, :, :], cosB[:, :, :], mul)
            P.tt("dve", TB2[:, :, :], S2[:, :, :], sinB[:, :, :], mul)
            P.tt("dve", TB3[:, :, :], S2[:, :, :], cosB[:, :, :], mul)
            P.tt("dve", TB4[:, :, :], S1[:, :, :], sinB[:, :, :], mul)
            P.tt("dve", HS[:, 0, :, :], TB1[:, :, :], TB2[:, :, :], sub)
            P.tt("dve", HS[:, 1, :, :], TB3[:, :, :], TB4[:, :, :], add)
            if kind == "p":
                g1l, g2l = S1[:, :, TC - 1], S2[:, :, TC - 1]
                P.tt("dve", S(2), g1l, cl[:, :], mul)
                P.tt("dve", S(3), g2l, sl[:, :], mul)
                P.tt("dve", S(4), g2l, cl[:, :], mul)
                P.tt("dve", S(5), g1l, sl[:, :], mul)
                P.tt("dve", hprev[0][:], S(2), S(3), sub)
                P.tt("dve", hprev[1][:], S(4), S(5), add)
            else:
                nb = TC // NST
                v4 = lambda t_: t_[:, :, :].rearrange("p q (b t) -> p q b t", t=NST)[:, :, :, NST - 1]
                g1l, g2l = v4(S1), v4(S2)
                clb = bc_last(c4[:, :, NST - 1:NST], nb)
                slb = bc_last(s4[:, :, NST - 1:NST], nb)
                ta, tb_ = G1[:, :, 0:nb], G1[:, :, nb:2 * nb]
                of = [hfin[ri][:, sc * nb:(sc + 1) * nb, :].rearrange("p b q -> p q b") for ri in range(2)]
                P.tt("dve", ta, g1l, clb, mul)
                P.tt("dve", tb_, g2l, slb, mul)
                P.tt("dve", of[0], ta, tb_, sub)
                P.tt("dve", ta, g2l, clb, mul)
                P.tt("dve", tb_, g1l, slb, mul)
                P.tt("dve", of[1], ta, tb_, add)
            psy = k.ps[6 + sc % 2]
            for ch in range(8):
                ycols = slice(ch * TC, (ch + 1) * TC)
                P.mm(psy[:, ycols], dd[:, ch, :], U[:, ch, u0:u0 + TC], start=True, stop=False)
                for q4 in range(4):
                    for ri in range(2):
                        P.mm(psy[32 * q4:32 * q4 + 32, ycols], CT[:, ri, ch, 32 * q4:32 * q4 + 32], HS[:, ri, 4 * ch + q4, :],
                             start=False, stop=(ri == 1), tile_position=(0, 32 * q4))
            P.act(G[:, :, u0:u0 + TC], psy[:, 0:8 * TC].rearrange("p (c t) -> p c t", t=TC), AF.Gelu_apprx_tanh)
        def gload(oc_):
            for hv_ in range(2):
                col0 = hv_ * D + oc_ * 128
                P.dma("pool", Wg[oc_ % 4][:, hv_, :, :], I["s5_w_glu"][j][:, col0:col0 + 128].rearrange("(kc k) m -> k kc m", k=128))

        for oc in range(4):
            gload(oc)
        for oc in range(8):
            Wc = Wg[oc % 4]
            psv, psg = (k.ps[4], k.ps[5]) if oc % 2 == 0 else (k.ps[0], k.ps[1])
            for hv_, ps in ((0, psv), (1, psg)):
                for kc in range(DC):
                    P.mm(ps[:, 0:n], Wc[:, hv_, kc, :], G[:, kc, 0:n], start=(kc == 0), stop=(kc == DC - 1))
            if oc + 4 < 8:
                gload(oc + 4)
            P.act(SG[:, 0:n], psg[:, 0:n], AF.Sigmoid)
            P.tt("dve", TM[:, 0:n], psv[:, 0:n], SG[:, 0:n], mul)
            P.tt("dve", k.XR[:, oc, c0:c0 + n], k.XR[:, oc, c0:c0 + n], TM[:, 0:n], add)
    for ri, (pn, sn) in enumerate((("p_s5_re", "s_s5_re"), ("p_s5_im", "s_s5_im"))):
        ps = k.ps[4 + ri]
        P.tr(ps[0:32, 0:128], hprev[ri][:], k.ident[:])
        P.copy("dve", k.stage[0:32, 0:128], ps[0:32, 0:128])
        P.dma("sp", O[pn][j].rearrange("(q g2) p -> q (g2 p)", g2=2), k.stage[0:32, 0:128])
        rows = O[sn][j].rearrange("b (q g2) p -> (b q) (g2 p)", g2=2)
        for blk in range(4):
            ps2 = k.ps[6 + blk % 2]
            P.tr(ps2[:, 0:128], hfin[ri][:, blk * 4:(blk + 1) * 4, :].rearrange("p b q -> p (b q)"), k.ident[:])
            P.copy("dve", k.stage2[:, blk * 128:(blk + 1) * 128], ps2[:, 0:128])
            P.dma("sp", rows[blk * 128:(blk + 1) * 128, :], k.stage2[:, blk * 128:(blk + 1) * 128])


def _lru(k, li, j):
    P, I, al = k.P, k.I, k.alloc
    k.ar_off = 8 * 1024
    W = [al([128, 3328], BF16) for _ in range(2)]
    lcw = al([128, 8, 10], F32)
    lcc = al([128, 2, 10], F32)
    diag = al([128, 2, 4, 128], BF16)
    xext = al([128, 2, 32 + 512], BF16)
    xexs = al([128, 2, NSB * 7], BF16)
    xcb = al([128, 2, 512], BF16)
    xcf = al([128, 2, 512], F32)
    rr = al([128, 2, 512], F32)
    ig = al([128, 2, 512], F32)
    aa = al([128, 2, 512], F32)
    a2 = al([128, 2, 512], F32)
    hh = al([128, 2, 512], F32)
    gg = al([128, 2, 512], F32)
    Yb = al([128, 2, 512], BF16)
    hprev = al([128, 10], F32)
    h0 = al([128, 10, NSB], F32)
    ltail = al([128, 10, 3 + 3 * NSB], F32)
    lht = al([128, 10, 1 + NSB], F32)
    lst = al([48, 128], F32)
    rows = [I["lru_conv_w"][j, r:r + 1, :] for r in range(4)] + [I["lru_conv_b"][j:j + 1, :], I["lru_b_gate_a"][j:j + 1, :],
                                                                I["lru_b_gate_x"][j:j + 1, :], I["lru_lambda"][j:j + 1, :]]
    load_cols(k, rows, LRU_W, lcw)
    load_cols(k, [I["st_lru"][j, b:b + 1, :] for b in range(NSB)], LRU_W, h0.rearrange("p c b -> p b c"))
    P.act(lcc[:, 0, :], lcw[:, 7, :], AF.Exp, scale=-1.0)
    P.act(lcc[:, 0, :], lcc[:, 0, :], AF.Ln, bias=k.onec[:, 0:1])
    P.ts("dve", lcc[:, 1, :], lcc[:, 0, :], -16.0, None, ALU.mult)
    P.ts("dve", lcc[:, 0, :], lcc[:, 0, :], -8.0, None, ALU.mult)
    P.memset("dve", hprev[:], 0.0)

    def load_w(c):
        Wc = W[c % 2]
        for half in range(2):
            col0 = half * LRU_W + c * 128
            src = I["lru_w_in"][j][:, col0:col0 + 128].rearrange("(kc k) m -> k kc m", k=128)
            P.dma("pool", Wc[:, half * 1024:(half + 1) * 1024].rearrange("k (kc m) -> k kc m", m=128), src)
        P.dma("pool", Wc[:, 2048:2176], I["lru_w_gate_a"][j, c])
        P.dma("pool", Wc[:, 2176:2304], I["lru_w_gate_x"][j, c])
        P.dma("pool", Wc[:, 2304:3328], I["lru_w_out"][j][c * 128:(c + 1) * 128, :])

    load_w(0)
    for c in range(LRU_C):
        if c + 1 < LRU_C:
            load_w(c + 1)
        Wc = W[c % 2]
        d = c % 2
        for t in range(4):
            P.ts("dve", diag[:, d, t, :], k.ident[:], lcw[:, t, c:c + 1], None, ALU.mult)
        P.memset("dve", xext[:, d, 29:32], 0.0)
        P.dma("sp", lst[:, :], I["st_lru_conv"][j].rearrange("b r f -> (b r) f")[:, c * 128:(c + 1) * 128])
        P.tr(k.ps[7][:, 0:48], lst[0:48, :], k.ident[0:48, 0:48])
        P.copy("dve", xexs[:, d, :].rearrange("p (b q) -> p b q", q=7)[:, :, 0:3],
               k.ps[7][:, 0:48].rearrange("p (b r) -> p b r", r=3))
        def stage_a(ti, c0, n, kind):
            last_p = (kind == "p" and c0 + n == NTP)
            e = ti % 2
            psx, psg, psc, psr, psi = k.ps[0], k.ps[1], k.ps[2], k.ps[3], k.ps[4]
            for kc in range(DC):
                P.mm(psx[:, 0:n], Wc[:, 1024 + kc * 128:1024 + (kc + 1) * 128], k.XN[:, kc, c0:c0 + n],
                     start=(kc == 0), stop=(kc == DC - 1))
            for kc in range(DC):
                P.mm(psg[:, 0:n], Wc[:, kc * 128:(kc + 1) * 128], k.XN[:, kc, c0:c0 + n],
                     start=(kc == 0), stop=(kc == DC - 1))
            if kind == "p":
                P.copy("act", xext[:, d, 32:32 + n], psx[:, 0:n])
                if last_p:
                    P.copy("act", ltail[:, c, 0:3], psx[:, n - 3:n])
                taps = [xext[:, d, 29 + t:29 + t + n] for t in range(4)]
            else:
                xv = xexs[:, d, :].rearrange("p (b q) -> p b q", q=7)
                pv = psx[:, 0:n].rearrange("p (b t) -> p b t", t=NST)
                P.copy("dve", xv[:, :, 3:7], pv)
                P.copy("dve", ltail[:, c, 3:3 + 3 * NSB].rearrange("p (b r) -> p b r", r=3), pv[:, :, 1:4])
                taps = [xv[:, :, t:t + NST] for t in range(4)]
            P.act(gg[:, e, 0:n], psg[:, 0:n], AF.Gelu_apprx_tanh)
            for t in range(4):
                P.mm(psc[:, 0:n], diag[:, d, t, :], taps[t], start=(t == 0), stop=(t == 3))
            if kind == "p" and not last_p:
                P.copy("pool", xext[:, d, 29:32], xext[:, d, 29 + n:32 + n])
            P.act(xcb[:, e, 0:n], psc[:, 0:n], AF.Identity, bias=lcw[:, 4, c:c + 1])
            P.act(xcf[:, e, 0:n], psc[:, 0:n], AF.Identity, bias=lcw[:, 4, c:c + 1])
            P.mm(psr[:, 0:n], Wc[:, 2048:2176], xcb[:, e, 0:n])
            P.mm(psi[:, 0:n], Wc[:, 2176:2304], xcb[:, e, 0:n])
            P.act(rr[:, e, 0:n], psr[:, 0:n], AF.Sigmoid, bias=lcw[:, 5, c:c + 1])
            P.act(ig[:, e, 0:n], psi[:, 0:n], AF.Sigmoid, bias=lcw[:, 6, c:c + 1])

        def stage_b(ti, c0, n, kind):
            last_p = (kind == "p" and c0 + n == NTP)
            e = ti % 2
            P.act(aa[:, e, 0:n], rr[:, e, 0:n], AF.Exp, scale=lcc[:, 0, c:c + 1])
            P.act(a2[:, e, 0:n], rr[:, e, 0:n], AF.Exp, scale=lcc[:, 1, c:c + 1])
            P.act(a2[:, e, 0:n], a2[:, e, 0:n], AF.Sqrt, bias=k.onec[:, 0:1], scale=-1.0)
            P.tt("dve", ig[:, e, 0:n], ig[:, e, 0:n], xcf[:, e, 0:n], ALU.mult)
            P.tt("dve", ig[:, e, 0:n], ig[:, e, 0:n], a2[:, e, 0:n], ALU.mult)
            if kind == "p":
                P.scan("dve", hh[:, e, 0:n], aa[:, e, 0:n], ig[:, e, 0:n], hprev[:, c:c + 1], ALU.mult, ALU.add)
                P.copy("dve", hprev[:, c:c + 1], hh[:, e, n - 1:n])
                if last_p:
                    P.copy("dve", lht[:, c, 0:1], hh[:, e, n - 1:n])
            else:
                av = aa[:, e, 0:n].rearrange("p (b t) -> p b t", t=NST)
                bv = ig[:, e, 0:n].rearrange("p (b t) -> p b t", t=NST)
                tmp = rr[:, e, 0:NSB]
                P.tt("dve", tmp, av[:, :, 0], h0[:, c, :], ALU.mult)
                P.tt("dve", bv[:, :, 0], bv[:, :, 0], tmp, ALU.add)
                P.memset("dve", av[:, :, 0], 0.0)
                P.scan("dve", hh[:, e, 0:n], aa[:, e, 0:n], ig[:, e, 0:n], 0.0, ALU.mult, ALU.add)
                P.copy("dve", lht[:, c, 1:1 + NSB], hh[:, e, 0:n].rearrange("p (b t) -> p b t", t=NST)[:, :, NST - 1])
            P.tt("dve", Yb[:, e, 0:n], gg[:, e, 0:n], hh[:, e, 0:n], ALU.mult)
            for oc in range(DC):
                ps = k.ps[5 + oc % 3]
                P.mm(ps[:, 0:n], Wc[:, 2304 + oc * 128:2304 + (oc + 1) * 128], Yb[:, e, 0:n])
                P.tt("dve", k.XR[:, oc, c0:c0 + n], k.XR[:, oc, c0:c0 + n], ps[:, 0:n], ALU.add)

        stage_a(0, *TILES[0])
        for ti in range(len(TILES)):
            if ti + 1 < len(TILES):
                stage_a(ti + 1, *TILES[ti + 1])
            stage_b(ti, *TILES[ti])
    _emit_tail(k, ltail, 10, k.O["p_lru_conv"][j], k.O["s_lru_conv"][j].rearrange("b r f -> (b r) f"), 3)
    _emit_tail(k, lht, 10, k.O["p_lru"][j:j + 1, :], k.O["s_lru"][j], 1)


GDN_TILES = [(i * 256, 256, "p") for i in range(8)] + [(NTP, NS, "s")]


def _gdn(k, li, j):
    P, I, O, al = k.P, k.I, k.O, k.alloc
    mul, add, sub = ALU.mult, ALU.add, ALU.subtract
    k.ar_off = 8 * 1024
    TN = 256
    UT = al([128, 128], F32)
    MT = al([128, 128], F32)
    MM = al([128, 128], F32)
    MTs = al([64, 64], F32)
    MMs = al([64, 64], F32)
    UTs = al([64, 64], F32)
    colm = al([128, NSB, 64], BF16)
    rowm = al([64, NSB], F32)
    lastm = al([64, 64], F32)
    cst = al([128, 2, 8], F32)
    nw = al([128, 1], F32)
    nwt = al([128, 1, 1], F32)
    gcw = al([128, 4, 24], F32)
    halo = al([128, 24, 3], BF16)
    gtail = al([128, 24, 3 + 3 * NSB], F32)
    onesf = al([128, 128], F32)
    ti_ = al([128, 128], I32)
    tf_ = al([128, 128], F32)
    tp_ = al([128, 2], F32)
    tpi = al([128, 2], I32)
    QK = al([128, 16, TN], BF16)
    V = al([128, 8, TN], BF16)
    Zs = al([128, 8, TN], BF16)
    AB = al([16, TN], F32)
    OT = al([128, 8, TN], BF16)
    S = al([128, 8, 128], F32)
    Sb = al([128, 8, 128], BF16)
    woff = k.ar_off
    Wp = [al([128, 8, 128], BF16) for _ in range(8)]
    Wab = al([128, 8, 16], BF16)
    dg = al([128, 2, 4, 128], BF16)
    xext = al([128, 2, 32 + TN], BF16)
    xexs = al([128, 2, NSB * 7], BF16)
    raw = al([128, 2, TN], BF16)
    sq = al([128, 2, TN], BF16)
    rs = al([128, 2, TN], F32)
    lst = al([48, 128], F32)
    k.ar_off = woff
    Dg = al([128, 4, 128], F32)
    Wo = [Dg.rearrange("p h c -> p (h c)").bitcast(BF16)[:, i * 1024:(i + 1) * 1024].rearrange("p (a b) -> p a b", b=128)
          for i in range(1)]
    ab = al([128, 16], F32)
    sm = al([128, 12, 8], F32)
    egl = al([128, 8, NSB], F32)
    DT = al([128, 4, 128], F32)
    Dm = al([128, 4, 128], F32)
    ktok = al([128, 4, 128], BF16)
    vtok = al([128, 4, 128], BF16)
    Nn = [al([128, 4, 128], F32) for _ in range(2)]
    NT = [al([128, 4, 128], F32) for _ in range(2)]
    X = [al([128, 4, 256], F32) for _ in range(2)]
    wT = al([128, 4, 128], BF16)
    vn = al([128, 4, 128], F32)
    vnb = al([128, 4, 128], BF16)
    aT = al([128, 4, 128], BF16)
    tq = al([128, 4, 128], F32)
    kd = al([128, 4, 128], BF16)
    o = al([128, 8, 128], F32)
    msk = al([128, NSB, 64], BF16)
    kdm = al([64, NSB, 128], BF16)
    Wo.append(al([128, 8, 128], BF16))
    Ssb = S.rearrange("p h d -> p (h d)").bitcast(BF16).rearrange("p (b d) -> p b d", d=128)
    Ss = Sb.rearrange("p h d -> p (h d)").bitcast(F32).rearrange("p (b d) -> p b d", d=128)
    assert k.ar_off <= 93 * 1024, k.ar_off
    BETA, NBETA, GG, CUM, NCUM, ECUM, BEC, DL, EDL, SS, RSTD, TMP = range(12)

    P.ts("dve", UT[:], k.jmp[:], 0.0, None, ALU.is_ge)
    P.ts("dve", MT[:], UT[:], -1.0, 1e4, add, mul)
    P.ts("dve", MM[:], UT[:], 1e4, None, mul)
    P.memset("dve", onesf[:], 1.0)
    P.add("pool", lambda e: e.iota(ti_[:], [[1, 128]], base=0, channel_multiplier=0), writes=[ti_[:]])
    P.add("dve", lambda e: e.tensor_single_scalar(ti_[:], ti_[:], 2, ALU.arith_shift_right), reads=[ti_[:]], writes=[ti_[:]])
    P.copy("dve", tf_[:], ti_[:])
    P.add("pool", lambda e: e.iota(tpi[:, 0:1], [[0, 1]], base=0, channel_multiplier=1), writes=[tpi[:, 0:1]])
    P.add("dve", lambda e: e.tensor_single_scalar(tpi[:, 1:2], tpi[:, 0:1], 2, ALU.arith_shift_right),
          reads=[tpi[:, 0:1]], writes=[tpi[:, 1:2]])
    P.copy("dve", tp_[:, 0:1], tpi[:, 1:2])
    same = Dg[0:64, 0, 0:64]
    P.ts("dve", same, tf_[0:64, 0:64], tp_[0:64, 0:1], None, ALU.is_equal)
    P.tt("dve", UTs[:], UT[0:64, 0:64], same, mul)
    P.ts("dve", MTs[:], UTs[:], -1.0, 1e4, add, mul)
    vld = Dg[0:64, 1, 0:64]
    P.ts("dve", vld, UT[0:64, 0:64], -1.0, 1.0, mul, add)
    P.tt("dve", vld, vld, same, mul)
    P.ts("dve", MMs[:], vld, -1.0, 1.0, mul, add)
    P.ts("dve", MMs[:], MMs[:], 1e4, None, mul)
    for b in range(NSB):
        P.ts("dve", colm[:, b, :], tf_[:, 0:64], float(b), None, ALU.is_equal)
    bidx = Dg[0:64, 2, 0:NSB]
    P.add("pool", lambda e: e.iota(ti_[0:64, 0:NSB], [[1, NSB]], base=0, channel_multiplier=0), writes=[ti_[0:64, 0:NSB]])
    P.copy("dve", bidx, ti_[0:64, 0:NSB])
    P.ts("dve", rowm[:], bidx, tp_[0:64, 0:1], None, ALU.is_equal)
    P.ts("dve", tp_[:, 1:2], tp_[:, 0:1], 4.0, 3.0, mul, add)
    P.add("pool", lambda e: e.iota(ti_[0:64, 0:64], [[1, 64]], base=0, channel_multiplier=0), writes=[ti_[0:64, 0:64]])
    P.copy("dve", Dg[0:64, 3, 0:64], ti_[0:64, 0:64])
    P.ts("dve", lastm[:], Dg[0:64, 3, 0:64], tp_[0:64, 1:2], None, ALU.is_equal)
    alog = I["gdn_a_log"][j]
    dtb = I["gdn_dt_bias"][j]
    P.dma("sp", cst[:, 0, :], bass.AP(alog.tensor, alog.offset, [[0, 128], [1, 8]]))
    P.dma("sp", cst[:, 1, :], bass.AP(dtb.tensor, dtb.offset, [[0, 128], [1, 8]]))
    P.act(cst[:, 0, :], cst[:, 0, :], AF.Exp)
    P.ts("dve", cst[:, 0, :], cst[:, 0, :], -1.0, None, mul)
    load_cols(k, [I["gdn_norm"][j:j + 1, :]], 128, nwt)
    P.copy("dve", nw[:], nwt[:, 0, :])
    load_cols(k, [I["gdn_conv_w"][j, r:r + 1, 0:2816] for r in range(4)], 2816, gcw[:, :, 0:22])
    load_cols(k, [I["gdn_conv_w"][j, r:r + 1, 2816:3072] for r in range(4)], 256, gcw[:, :, 22:24])
    P.memset("dve", halo[:], 0.0)
    P.memset("dve", S[:], 0.0)
    P.memset("dve", Sb[:], 0.0)

    def proj_tile(c0, n, kind):
        last_p = (kind == "p" and c0 + n == NTP)
        P.dma("pool", Wab[:, :, :], I["gdn_w_in"][j][:, 4096:4112].rearrange("(kc k) m -> k kc m", k=128))
        for kc in range(DC):
            P.mm(k.ps[3][0:16, 0:n], Wab[:, kc, :], k.XN[:, kc, c0:c0 + n], start=(kc == 0), stop=(kc == DC - 1))
        P.copy("dve", AB[:, 0:n], k.ps[3][0:16, 0:n])
        taps_of = {}

        def wload(ch):
            P.dma("pool", Wp[ch % 8][:, :, :], I["gdn_w_in"][j][:, ch * 128:(ch + 1) * 128].rearrange("(kc k) m -> k kc m", k=128))

        for ch0 in range(8):
            wload(ch0)

        def s1(ch):
            Wc = Wp[ch % 8]
            d = ch % 2
            psx = k.ps[ch % 2]
            for kc in range(DC):
                P.mm(psx[:, 0:n], Wc[:, kc, :], k.XN[:, kc, c0:c0 + n], start=(kc == 0), stop=(kc == DC - 1))
            if ch + 8 < 32:
                wload(ch + 8)
            if ch >= 24:
                P.act(Zs[:, ch - 24, 0:n], psx[:, 0:n], AF.Silu)
                return
            for t in range(4):
                P.ts("dve", dg[:, d, t, :], k.ident[:], gcw[:, t, ch:ch + 1], None, mul)
            if kind == "p":
                P.copy("pool", xext[:, d, 29:32], halo[:, ch, :])
                P.copy("act", xext[:, d, 32:32 + n], psx[:, 0:n])
                if last_p:
                    P.copy("act", gtail[:, ch, 0:3], psx[:, n - 3:n])
                else:
                    P.copy("pool", halo[:, ch, :], xext[:, d, 29 + n:32 + n])
                taps_of[ch] = [xext[:, d, 29 + t:29 + t + n] for t in range(4)]
            else:
                P.dma("sp", lst[:, :], I["st_gdn_conv"][j].rearrange("b r f -> (b r) f")[:, ch * 128:(ch + 1) * 128])
                P.tr(k.ps[7][:, 0:48], lst[0:48, :], k.ident[0:48, 0:48])
                xv = xexs[:, d, :].rearrange("p (b q) -> p b q", q=7)
                P.copy("dve", xv[:, :, 0:3], k.ps[7][:, 0:48].rearrange("p (b r) -> p b r", r=3))
                pv = psx[:, 0:n].rearrange("p (b t) -> p b t", t=NST)
                P.copy("dve", xv[:, :, 3:7], pv)
                P.copy("dve", gtail[:, ch, 3:3 + 3 * NSB].rearrange("p (b r) -> p b r", r=3), pv[:, :, 1:4])
                taps_of[ch] = [xv[:, :, t:t + NST] for t in range(4)]

        def s2(ch):
            if ch >= 24:
                return
            d = ch % 2
            psc = k.ps[2 if ch % 2 == 0 else 5]
            taps = taps_of[ch]
            for t in range(4):
                P.mm(psc[:, 0:n], dg[:, d, t, :], taps[t], start=(t == 0), stop=(t == 3))
            if ch >= 16:
                P.act(V[:, ch - 16, 0:n], psc[:, 0:n], AF.Silu)
                return
            P.act(raw[:, d, 0:n], psc[:, 0:n], AF.Silu)
            P.act(sq[:, d, 0:n], raw[:, d, 0:n], AF.Square)
            pss = k.ps[3 if ch % 2 == 0 else 6]
            P.mm(pss[:, 0:n], k.onesb[:], sq[:, d, 0:n])

        def s3(ch):
            if ch >= 16:
                return
            d = ch % 2
            pss = k.ps[3 if ch % 2 == 0 else 6]
            P.act(rs[:, d, 0:n], pss[:, 0:n], AF.Sqrt, bias=k.epsc[:, 0:1])
            P.recip("dve", rs[:, d, 0:n], rs[:, d, 0:n])
            if ch < 8:
                P.stt("dve", QK[:, ch, 0:n], raw[:, d, 0:n], float(128 ** -0.5), rs[:, d, 0:n], mul, mul)
            else:
                P.tt("dve", QK[:, ch, 0:n], raw[:, d, 0:n], rs[:, d, 0:n], mul)

        NCH = 32
        for it in range(NCH + 2):
            if it < NCH:
                s1(it)
            if 0 <= it - 1 < NCH:
                s2(it - 1)
            if 0 <= it - 2 < NCH:
                s3(it - 2)

    def chunk(t0, C, nseq):
        cols = slice(t0, t0 + C)
        sample = nseq > 1
        idC = k.ident[0:C, 0:C]
        sv = lambda i: sm[0:C, i, :]
        P.tr(k.ps[7][0:C, 0:16], AB[0:16, cols], k.ident[0:16, 0:16])
        P.copy("dve", ab[0:C, :], k.ps[7][0:C, 0:16])
        P.act(sv(BETA), ab[0:C, 8:16], AF.Sigmoid)
        P.ts("dve", sv(NBETA), sv(BETA), -1.0, None, mul)
        P.tt("dve", sv(TMP), ab[0:C, 0:8], cst[0:C, 1, :], add)
        P.act(sv(TMP), sv(TMP), AF.Exp)
        P.act(sv(TMP), sv(TMP), AF.Ln, bias=k.onec[0:C, 0:1])
        P.tt("dve", sv(GG), sv(TMP), cst[0:C, 0, :], mul)
        P.mm(k.ps[6][0:C, 0:8], (UTs[:, :] if sample else UT[:, :]), sv(GG))
        P.copy("dve", sv(CUM), k.ps[6][0:C, 0:8])
        P.ts("dve", sv(NCUM), sv(CUM), -1.0, None, mul)
        P.act(sv(ECUM), sv(CUM), AF.Exp)
        P.tt("dve", sv(BEC), sv(BETA), sv(ECUM), mul)
        mt = MTs[:, :] if sample else MT[:, :]
        mm_ = MMs[:, :] if sample else MM[:, :]
        for hg in range(2):
            hs = [hg * 4 + i for i in range(4)]
            psA, psB, psC, psD, psR = k.ps[0], k.ps[1], k.ps[2], k.ps[3], k.ps[4]
            P.tt("dve", Dg[0:C, :, 0:C], ins_bc(idC, 1, 4),
                 bc_last(sm[0:C, CUM, hg * 4:hg * 4 + 4].rearrange("p (h o) -> p h o", o=1), C), mul)
            P.mm(psR[:, 0:4 * C].rearrange("p (h c) -> p h c", c=C), onesf[0:C, :], Dg[0:C, :, 0:C])

            def Rv(i, rows=C):
                return psR[0:rows, i * C:(i + 1) * C]

            for i, h in enumerate(hs):
                P.tt("dve", DT[0:C, i, 0:C], Rv(i), mt, add)
                P.tt("dve", Dm[0:C, i, 0:C], Rv(i), mm_, add)
            for i, h in enumerate(hs):
                P.act(DT[0:C, i, 0:C], DT[0:C, i, 0:C], AF.Exp, bias=sm[0:C, NCUM, h:h + 1])
                P.act(Dm[0:C, i, 0:C], Dm[0:C, i, 0:C], AF.Exp, bias=sm[0:C, CUM, h:h + 1], scale=-1.0)
            if not sample:
                for i, h in enumerate(hs):
                    P.tt("dve", sm[0:C, DL, h:h + 1], Rv(i)[:, C - 1:C], sm[0:C, CUM, h:h + 1], sub)
                    P.act(egl[:, h, 0:1], Rv(i, 128)[:, C - 1:C], AF.Exp)
            else:
                for i, h in enumerate(hs):
                    P.tt("dve", Dg[0:C, i, 0:C], Rv(i), lastm[:, :], mul)
                    P.add("dve", lambda e, i=i, h=h: e.reduce_sum(sm[0:C, TMP, h:h + 1], Dg[0:C, i, 0:C], AX.X),
                          reads=[Dg[0:C, i, 0:C]], writes=[sm[0:C, TMP, h:h + 1]])
                    P.act(egl[:, h, :], Rv(i, 128).rearrange("p (b t) -> p b t", t=NST)[:, :, NST - 1], AF.Exp)
                P.tt("dve", sm[0:C, DL, hg * 4:hg * 4 + 4], sm[0:C, TMP, hg * 4:hg * 4 + 4], sm[0:C, CUM, hg * 4:hg * 4 + 4], sub)
            P.act(sm[0:C, EDL, hg * 4:hg * 4 + 4], sm[0:C, DL, hg * 4:hg * 4 + 4], AF.Exp)
            pT = psA[:, :].bitcast(BF16)
            for i, h in enumerate(hs):
                P.tr(pT[0:C, i * 128:(i + 1) * 128], QK[:, 8 + h, cols], k.identb[:])
                P.tr(pT[0:C, 512 + i * 128:512 + (i + 1) * 128], V[:, h, cols], k.identb[:])
            P.copy("act", ktok[0:C, :, :], pT[0:C, 0:512].rearrange("p (h d) -> p h d", d=128))
            P.copy("act", vtok[0:C, :, :], pT[0:C, 512:1024].rearrange("p (h d) -> p h d", d=128))
            for i, h in enumerate(hs):
                P.mm(psB[0:C, i * C:(i + 1) * C], QK[:, 8 + h, cols], QK[:, 8 + h, cols])
            for i, h in enumerate(hs):
                P.stt("dve", Nn[0][0:C, i, 0:C], psB[0:C, i * C:(i + 1) * C], sm[0:C, NBETA, h:h + 1], Dm[0:C, i, 0:C], mul, mul)
            for i, h in enumerate(hs):
                P.tr(psC[0:C, i * C:(i + 1) * C], Nn[0][0:C, i, 0:C], idC)
            P.copy("act", NT[0][0:C, :, 0:C], psC[0:C, 0:4 * C].rearrange("p (h c) -> p h c", c=C))
            for i, h in enumerate(hs):
                P.ts("dve", X[0][0:C, i, 0:128], vtok[0:C, i, :], sm[0:C, BETA, h:h + 1], None, mul)
                P.ts("dve", X[0][0:C, i, 128:256], ktok[0:C, i, :], sm[0:C, BEC, h:h + 1], None, mul)
            nlev = 2 if sample else 7
            cur = 0
            for lev in range(nlev):
                for i in range(4):
                    P.mm(k.ps[2 + i // 2][0:C, (i % 2) * 256:(i % 2 + 1) * 256], NT[cur][0:C, i, 0:C], X[lev % 2][0:C, i, :])
                for half in range(2):
                    P.tt("dve", X[(lev + 1) % 2][0:C, 2 * half:2 * half + 2, :], X[lev % 2][0:C, 2 * half:2 * half + 2, :],
                         k.ps[2 + half][0:C, 0:512].rearrange("p (h d) -> p h d", d=256), add)
                if lev < nlev - 1:
                    for i in range(4):
                        P.mm(psA[0:C, i * C:(i + 1) * C], NT[cur][0:C, i, 0:C], Nn[cur][0:C, i, 0:C])
                        P.mm(psB[0:C, i * C:(i + 1) * C], Nn[cur][0:C, i, 0:C], NT[cur][0:C, i, 0:C])
                    P.copy("act", Nn[1 - cur][0:C, :, 0:C], psA[0:C, 0:4 * C].rearrange("p (h c) -> p h c", c=C))
                    P.copy("act", NT[1 - cur][0:C, :, 0:C], psB[0:C, 0:4 * C].rearrange("p (h c) -> p h c", c=C))
                    cur = 1 - cur
            Xs = X[nlev % 2]
            for i in range(4):
                P.tr(psA[:, i * C:(i + 1) * C], Xs[0:C, i, 128:256], idC)
            P.copy("act", wT[:, :, 0:C], psA[:, 0:4 * C].rearrange("p (h c) -> p h c", c=C))
            if not sample:
                for i, h in enumerate(hs):
                    P.mm(psB[0:C, i * 128:(i + 1) * 128], wT[:, i, 0:C], Sb[:, h, :])
                    P.mm(psC[0:C, i * 128:(i + 1) * 128], QK[:, h, cols], Sb[:, h, :])
                sample_ws = None
            else:
                for i, h in enumerate(hs):
                    P.dma("pool", Ssb[:, :, :], I["st_gdn"][j][:, h, :, :].rearrange("b k v -> k b v"))
                    P.tt("dve", msk[:, :, :], ins_bc(wT[:, i, 0:C], 1, NSB), colm[:, :, :], mul)
                    for b in range(NSB):
                        P.mm(psB[0:C, i * 128:(i + 1) * 128], msk[:, b, :], Ssb[:, b, :], start=(b == 0), stop=(b == NSB - 1))
                    P.tt("dve", msk[:, :, :], ins_bc(QK[:, h, cols], 1, NSB), colm[:, :, :], mul)
                    for b in range(NSB):
                        P.mm(psC[0:C, i * 128:(i + 1) * 128], msk[:, b, :], Ssb[:, b, :], start=(b == 0), stop=(b == NSB - 1))
            P.tt("dve", vn[0:C, :, :], Xs[0:C, :, 0:128], psB[0:C, 0:512].rearrange("p (h d) -> p h d", d=128), sub)
            P.copy("act", vnb[0:C, :, :], vn[0:C, :, :])
            for i, h in enumerate(hs):
                P.act(tq[0:C, i, :], psC[0:C, i * 128:(i + 1) * 128], AF.Identity, scale=sm[0:C, ECUM, h:h + 1])
            for i, h in enumerate(hs):
                P.mm(psD[0:C, i * C:(i + 1) * C], QK[:, 8 + h, cols], QK[:, h, cols])
            P.tt("dve", aT[0:C, :, 0:C], psD[0:C, 0:4 * C].rearrange("p (h c) -> p h c", c=C), DT[0:C, :, 0:C], mul)
            for i, h in enumerate(hs):
                P.mm(psB[0:C, i * 128:(i + 1) * 128], aT[0:C, i, 0:C], vnb[0:C, i, :])
            P.tt("dve", o[0:C, hg * 4:hg * 4 + 4, :], tq[0:C, :, :], psB[0:C, 0:512].rearrange("p (h d) -> p h d", d=128), add)
            P.tt("dve", kd[0:C, :, :], ktok[0:C, :, :],
                 bc_last(sm[0:C, EDL, hg * 4:hg * 4 + 4].rearrange("p (h o) -> p h o", o=1), 128), mul)
            if not sample:
                for i, h in enumerate(hs):
                    P.mm(psC[:, i * 128:(i + 1) * 128], kd[0:C, i, :], vnb[0:C, i, :])
                for i, h in enumerate(hs):
                    P.stt("dve", S[:, h, :], S[:, h, :], egl[:, h, 0:1], psC[:, i * 128:(i + 1) * 128], mul, add)
                P.copy("act", Sb[:, hg * 4:hg * 4 + 4, :], S[:, hg * 4:hg * 4 + 4, :])
            else:
                for i, h in enumerate(hs):
                    P.tt("dve", kdm[:, :, :], ins_bc(kd[0:C, i, :], 1, NSB),
                         bc_last(rowm[:, :].rearrange("p (b o) -> p b o", o=1), 128), mul)
                    for b4 in range(NSB // 4):
                        P.dma("sp", Ss[:, :, :], I["st_gdn"][j][b4 * 4:(b4 + 1) * 4, h, :, :].rearrange("b k v -> k b v"))
                        for bb in range(4):
                            P.mm(psC[:, bb * 128:(bb + 1) * 128], kdm[:, b4 * 4 + bb, :], vnb[0:C, i, :])
                        for bb in range(4):
                            b = b4 * 4 + bb
                            P.stt("dve", Ss[:, bb, :], Ss[:, bb, :], egl[:, h, b:b + 1], psC[:, bb * 128:(bb + 1) * 128], mul, add)
                        P.dma("sp", O["s_gdn"][j][b4 * 4:(b4 + 1) * 4, h, :, :].rearrange("b k v -> k b v"), Ss[:, :, :])
        for h in range(8):
            P.add("act", lambda e, h=h: e.activation(X[0][0:C, h % 4, 0:128], o[0:C, h, :], AF.Square, accum_out=sm[0:C, SS, h:h + 1]),
                  reads=[o[0:C, h, :]], writes=[X[0][0:C, h % 4, 0:128], sm[0:C, SS, h:h + 1]])
        P.act(sv(RSTD), sv(SS), AF.Sqrt, bias=k.epsc[0:C, 0:1], scale=1.0 / 128)
        P.recip("dve", sv(RSTD), sv(RSTD))
        P.tt("dve", o[0:C, :, :], o[0:C, :, :], bc_last(sv(RSTD).rearrange("p (h o) -> p h o", o=1), 128), mul)
        for h in range(8):
            ps = k.ps[h // 4]
            P.tr(ps[:, (h % 4) * C:(h % 4 + 1) * C], o[0:C, h, :], idC)
        for hg in range(2):
            P.stt("dve", OT[:, hg * 4:hg * 4 + 4, cols], k.ps[hg][:, 0:4 * C].rearrange("p (h c) -> p h c", c=C), nw[:, 0:1],
                  Zs[:, hg * 4:hg * 4 + 4, cols], mul, mul)

    for ti, (c0, n, kind) in enumerate(GDN_TILES):
        proj_tile(c0, n, kind)
        if kind == "p":
            for cc in range(n // 128):
                chunk(cc * 128, 128, 1)
            if c0 + n == NTP:
                P.dma("sp", O["p_gdn"][j].rearrange("h k v -> k h v"), S[:, :, :])
        else:
            chunk(0, NS, NSB)
        def oload(oc_):
            P.dma("pool", Wo[oc_ % 2][:, :, :], I["gdn_w_out"][j][:, oc_ * 128:(oc_ + 1) * 128].rearrange("(kc k) m -> k kc m", k=128))

        oload(0)
        for oc in range(8):
            if oc + 1 < 8:
                oload(oc + 1)
            Wc = Wo[oc % 2]
            ps = k.ps[6 + oc % 2]
            for kc in range(8):
                P.mm(ps[:, 0:n], Wc[:, kc, :], OT[:, kc, 0:n], start=(kc == 0), stop=(kc == 7))
            P.tt("dve", k.XR[:, oc, c0:c0 + n], k.XR[:, oc, c0:c0 + n], ps[:, 0:n], add)
    _emit_tail(k, gtail, 24, O["p_gdn_conv"][j], O["s_gdn_conv"][j].rearrange("b r f -> (b r) f"), 3)


_WNAMES = ["norm_mix", "norm_ffn", "norm_final", "s5_w_in", "s5_a_re", "s5_a_im", "s5_log_dt", "s5_b_re", "s5_b_im",
           "s5_c_re", "s5_c_im", "s5_d", "s5_w_glu", "lru_w_in", "lru_conv_w", "lru_conv_b", "lru_w_gate_a",
           "lru_b_gate_a", "lru_w_gate_x", "lru_b_gate_x", "lru_lambda", "lru_w_out", "gdn_w_in", "gdn_conv_w",
           "gdn_a_log", "gdn_dt_bias", "gdn_norm", "gdn_w_out", "ffn_w_up", "ffn_conv_w", "ffn_conv_b", "ffn_w_down"]


def make_in_maps(inp):
    maps = []
    c = np.ascontiguousarray
    for core in range(8):
        sl = slice(core * NSB, (core + 1) * NSB)
        m = {
            "xp": c(inp["x_prompt"][core]),
            "xs": c(inp["x_sample"][sl].reshape(NS, D)),
            "st_s5_re": c(inp["state_s5_re"][:, sl]),
            "st_s5_im": c(inp["state_s5_im"][:, sl]),
            "st_lru": c(inp["state_lru"][:, sl]),
            "st_lru_conv": c(inp["state_lru_conv"][:, sl]),
            "st_gdn": c(inp["state_gdn"][:, sl]),
            "st_gdn_conv": c(inp["state_gdn_conv"][:, sl]),
            "st_ffn_conv": c(inp["state_ffn_conv"][:, sl]),
        }
        for n_ in _WNAMES:
            m[n_] = c(np.asarray(inp[n_], dtype=np.float32))
        maps.append(m)
    return maps


_NC_CACHE = {}


def kernel(**inputs):
    inp = {k_: np.asarray(v) for k_, v in inputs.items()}
    if "nc" not in _NC_CACHE:
        _NC_CACHE["nc"] = build()
    nc = _NC_CACHE["nc"]
    in_maps = make_in_maps(inp)
    res = run_bass_kernel_spmd(nc, in_maps, core_ids=list(range(8)))
    R = res.results

    def cat_p(name):
        return np.stack([np.asarray(r[name], dtype=np.float32) for r in R], axis=1)

    def cat_s(name):
        return np.concatenate([np.asarray(r[name], dtype=np.float32) for r in R], axis=1)

    y_prompt = np.stack([np.asarray(r["yp"], dtype=np.float32) for r in R], axis=0)
    y_sample = np.concatenate([np.asarray(r["ys"], dtype=np.float32).reshape(NSB, NST, D) for r in R], axis=0)
    return (y_prompt, y_sample,
            cat_p("p_s5_re"), cat_p("p_s5_im"), cat_p("p_lru"), cat_p("p_lru_conv"), cat_p("p_gdn"),
            cat_p("p_gdn_conv"), cat_p("p_ffn_conv"),
            cat_s("s_s5_re"), cat_s("s_s5_im"), cat_s("s_lru"), cat_s("s_lru_conv"), cat_s("s_gdn"),
            cat_s("s_gdn_conv"), cat_s("s_ffn_conv"))
```
